# Optimizing a Trainium2 kernel written in Bass

```python
import jax, jax.numpy as jnp
from jax import lax
import numpy as np

D_MODEL = 1024
BATCH = 8
SEQ = 4096
DEPTH = 2
DEC_BATCH = 128
DEC_SEQ = 8
PAST_LEN = 16384
PAGE_SIZE = 128

HEAD_DIM = 64
W_A = D_MODEL // 2
N_Q_HEADS = W_A // HEAD_DIM
N_KV_HEADS = 2
Q_PER_KV = N_Q_HEADS // N_KV_HEADS
W_KV = N_KV_HEADS * HEAD_DIM
WINDOW = 128
ROT_DIM = HEAD_DIM // 4
ROPE_THETA = 500000.0
MASK_VALUE = -1e30
CHUNK = 128
W_B = D_MODEL // 2
N_SG_GROUPS = 8
SG_GROUP_DIM = W_B // N_SG_GROUPS
W_C = D_MODEL // 2
HGRN_HEADS = 4
HGRN_DK = W_C // HGRN_HEADS
HGRN_CHUNK = 64
IN_SIZES = (W_A, W_KV, W_KV, W_A, W_B, W_B, W_B, W_C, W_C, W_C, W_C, D_MODEL, D_MODEL, D_MODEL)
IN_WIDTH = 2 * W_A + 2 * W_KV + 3 * W_B + 4 * W_C + 3 * D_MODEL
EPS = 1e-6

kernel_name = 'hybrid_swa_gmlp_hgrn2_step'


def _rmsnorm(x, g):
    xf = x.astype(jnp.float32)
    y = xf * lax.rsqrt(jnp.mean(xf * xf, axis=-1, keepdims=True) + EPS)
    return y.astype(x.dtype) * g


def _split_cols(a, sizes):
    out, off = [], 0
    for s in sizes:
        out.append(a[..., off:off + s])
        off += s
    return out


def _partial_rope(x, pos):
    half = ROT_DIM // 2
    inv = ROPE_THETA ** (-jnp.arange(half, dtype=jnp.float32) / half)
    ang = pos.astype(jnp.float32)[:, None] * inv[None, :]
    cos = jnp.cos(ang)[None, :, None, :].astype(x.dtype)
    sin = jnp.sin(ang)[None, :, None, :].astype(x.dtype)
    x1, x2, rest = x[..., :half], x[..., half:ROT_DIM], x[..., ROT_DIM:]
    return jnp.concatenate([x1 * cos - x2 * sin, x1 * sin + x2 * cos, rest], axis=-1)


def _window_attention(q, k, v, buf_k, buf_v, start, sinks):
    b, t = q.shape[:2]
    qb_len = WINDOW if t >= WINDOW else t
    n_blk = -(-t // qb_len)
    tp = n_blk * qb_len
    pad = [(0, 0), (0, tp - t), (0, 0), (0, 0)]
    buf_k = buf_k.astype(k.dtype)
    buf_v = buf_v.astype(v.dtype)
    k_all = jnp.concatenate([buf_k, jnp.pad(k, pad)], axis=1)
    v_all = jnp.concatenate([buf_v, jnp.pad(v, pad)], axis=1)
    kb_len = WINDOW + qb_len
    idx = np.arange(n_blk)[:, None] * qb_len + np.arange(kb_len)[None, :]
    k_blk = k_all[:, idx]
    v_blk = v_all[:, idx]
    q_blk = jnp.pad(q, pad).reshape(b, n_blk, qb_len, N_KV_HEADS, Q_PER_KV, HEAD_DIM)
    s = jnp.einsum('bnqkgd,bnskd->bnkgqs', q_blk, k_blk).astype(jnp.float32) * (HEAD_DIM ** -0.5)
    dist = WINDOW + np.arange(qb_len)[:, None] - np.arange(kb_len)[None, :]
    k_pos = start - WINDOW + idx
    valid = ((dist >= 0) & (dist < WINDOW))[None] & (k_pos >= 0)[:, None, :]
    valid = valid[None, :, None, None]
    s = jnp.where(valid, s, MASK_VALUE)
    sink = sinks.astype(jnp.float32).reshape(N_KV_HEADS, Q_PER_KV)[None, None, :, :, None, None]
    m = jnp.maximum(jnp.max(s, axis=-1, keepdims=True), sink)
    p = jnp.where(valid, jnp.exp(s - m), 0.0)
    denom = jnp.sum(p, axis=-1, keepdims=True) + jnp.exp(sink - m)
    o = jnp.einsum('bnkgqs,bnskd->bnqkgd', (p / denom).astype(v.dtype), v_blk)
    o = o.reshape(b, tp, W_A)[:, :t]
    new_k = jnp.concatenate([buf_k, k], axis=1)[:, -WINDOW:]
    new_v = jnp.concatenate([buf_v, v], axis=1)[:, -WINDOW:]
    return o, new_k, new_v


def _spatial_gating(u, v, w_spatial, b_spatial):
    b, t = u.shape[:2]
    n_c = -(-t // CHUNK)
    tp = n_c * CHUNK
    vc = jnp.pad(v, [(0, 0), (0, tp - t), (0, 0)]).reshape(b, n_c, CHUNK, N_SG_GROUPS, SG_GROUP_DIM)
    z = jnp.einsum('gts,bcsgd->bctgd', jnp.tril(w_spatial), vc) + b_spatial.T[None, None, :, :, None]
    return u * z.reshape(b, tp, W_B)[:, :t]


def _hgrn_lower_bounds(lb_param):
    p = jax.nn.softmax(lb_param.astype(jnp.float32), axis=0)
    return jnp.cumsum(p, axis=0) - p[0:1]


def _hgrn2(q, log_f, i_in, s0):
    b, t = q.shape[:2]
    f32 = jnp.float32
    L = min(HGRN_CHUNK, t)
    n_c = -(-t // L)
    tp = n_c * L
    k = -jnp.expm1(log_f)

    def chunks(a):
        a = jnp.pad(a.astype(f32), [(0, 0), (0, tp - t), (0, 0), (0, 0)])
        return a.reshape(b, n_c, L, HGRN_HEADS, HGRN_DK).transpose(1, 0, 3, 2, 4)

    qc = chunks(q.astype(f32) * (HGRN_DK ** -0.5))
    kc, vc, gc = chunks(k), chunks(i_in), chunks(log_f)
    causal = np.tril(np.ones((L, L), dtype=bool))[:, :, None]

    def step(S, blk):
        qb, kb, vb, gb = blk
        G = jnp.cumsum(gb, axis=2)
        inter = jnp.einsum('bhtk,bhkv->bhtv', qb * jnp.exp(G), S)
        diff = G[:, :, :, None, :] - G[:, :, None, :, :]
        decay = jnp.where(causal, jnp.exp(jnp.where(causal, diff, 0.0)), 0.0)
        attn = jnp.einsum('bhtk,bhtsk,bhsk->bhts', qb, decay, kb)
        intra = jnp.einsum('bhts,bhsv->bhtv', attn, vb)
        g_last = G[:, :, -1]
        S = jnp.exp(g_last)[..., None] * S + jnp.einsum(
            'bhsk,bhsv->bhkv', kb * jnp.exp(g_last[:, :, None] - G), vb)
        return S, inter + intra

    s_fin, o = lax.scan(step, s0.astype(f32), (qc, kc, vc, gc))
    o = o.transpose(1, 0, 3, 2, 4).reshape(b, tp, HGRN_HEADS, HGRN_DK)[:, :t]
    return o.astype(q.dtype), s_fin.astype(s0.dtype)


def _trunk_layer(x, c, start, buf_k, buf_v, s0, lb, norm_g, w_ada, b_ada, w_in, q_norm_g, k_norm_g,
                 sinks, sg_norm_g, w_spatial, b_spatial, hgrn_norm_g, w_br_a, w_br_b, w_br_c, w_out):
    b, t, _ = x.shape
    shift, scale, gate = jnp.split(jax.nn.silu(c) @ w_ada + b_ada, 3, axis=-1)
    h = _rmsnorm(x, norm_g) * (1.0 + scale[:, None]) + shift[:, None]
    (qa, ka, va, ga, ub, vb, gb, qc, fc, ic, gc, ma, mb, mc) = _split_cols(h @ w_in, IN_SIZES)
    pos = start + jnp.arange(t)
    qa = _partial_rope(_rmsnorm(qa.reshape(b, t, N_Q_HEADS, HEAD_DIM), q_norm_g), pos)
    ka = _partial_rope(_rmsnorm(ka.reshape(b, t, N_KV_HEADS, HEAD_DIM), k_norm_g), pos)
    va = va.reshape(b, t, N_KV_HEADS, HEAD_DIM)
    ya, new_k, new_v = _window_attention(qa, ka, va, buf_k, buf_v, start, sinks)
    vb = _rmsnorm(vb, sg_norm_g)
    yb = _spatial_gating(ub, vb, w_spatial, b_spatial)
    f_gate = lb + (1.0 - lb) * jax.nn.sigmoid(fc.astype(jnp.float32))
    log_f = jnp.log(jnp.maximum(f_gate, jnp.finfo(jnp.float32).tiny))
    yc, s_new = _hgrn2(qc.reshape(b, t, HGRN_HEADS, HGRN_DK),
                       log_f.reshape(b, t, HGRN_HEADS, HGRN_DK),
                       ic.reshape(b, t, HGRN_HEADS, HGRN_DK), s0)
    yc = _rmsnorm(yc, hgrn_norm_g).reshape(b, t, W_C)
    merged = (jax.nn.sigmoid(ma) * ((ya * jax.nn.silu(ga)) @ w_br_a)
              + jax.nn.sigmoid(mb) * ((yb * jax.nn.silu(gb)) @ w_br_b)
              + jax.nn.sigmoid(mc) * ((yc * jax.nn.silu(gc)) @ w_br_c))
    x = x + gate[:, None] * (merged @ w_out)
    return x, new_k, new_v, s_new, vb


def setup_inputs(seed: int = 0) -> dict:
    key = jax.random.key(seed)
    ks = jax.random.split(key, 24)

    def nrm(k, shape, scale=1.0):
        return scale * jax.random.normal(k, shape, jnp.float32)

    return {
        'x_prompt': nrm(ks[0], (BATCH, SEQ, D_MODEL)),
        'x_sample': nrm(ks[1], (DEC_BATCH, DEC_SEQ, D_MODEL)),
        'cache_win_k': nrm(ks[2], (DEPTH, DEC_BATCH, WINDOW, N_KV_HEADS, HEAD_DIM)),
        'cache_win_v': nrm(ks[3], (DEPTH, DEC_BATCH, WINDOW, N_KV_HEADS, HEAD_DIM)),
        'state_hgrn': nrm(ks[4], (DEPTH, DEC_BATCH, HGRN_HEADS, HGRN_DK, HGRN_DK), 0.5),
        'c_prompt': nrm(ks[5], (BATCH, D_MODEL)),
        'c_sample': nrm(ks[6], (DEC_BATCH, D_MODEL)),
        'norm_g': 1.0 + nrm(ks[7], (DEPTH, D_MODEL), 0.05),
        'w_ada': nrm(ks[8], (DEPTH, D_MODEL, 3 * D_MODEL), D_MODEL ** -0.5),
        'b_ada': nrm(ks[9], (DEPTH, 3 * D_MODEL), 0.02),
        'w_in': nrm(ks[10], (DEPTH, D_MODEL, IN_WIDTH), D_MODEL ** -0.5),
        'q_norm_g': 1.0 + nrm(ks[11], (DEPTH, HEAD_DIM), 0.05),
        'k_norm_g': 1.0 + nrm(ks[12], (DEPTH, HEAD_DIM), 0.05),
        'sinks': nrm(ks[13], (DEPTH, N_Q_HEADS), 0.5),
        'sg_norm_g': 1.0 + nrm(ks[14], (DEPTH, W_B), 0.05),
        'w_spatial': nrm(ks[15], (DEPTH, N_SG_GROUPS, CHUNK, CHUNK), CHUNK ** -0.5),
        'b_spatial': 1.0 + nrm(ks[16], (DEPTH, N_SG_GROUPS, CHUNK), 0.02),
        'hgrn_lb': nrm(ks[17], (DEPTH, W_C)),
        'hgrn_norm_g': 1.0 + nrm(ks[18], (DEPTH, HGRN_DK), 0.05),
        'w_br_a': nrm(ks[19], (DEPTH, W_A, D_MODEL), W_A ** -0.5),
        'w_br_b': nrm(ks[20], (DEPTH, W_B, D_MODEL), W_B ** -0.5),
        'w_br_c': nrm(ks[21], (DEPTH, W_C, D_MODEL), W_C ** -0.5),
        'w_out': nrm(ks[22], (DEPTH, D_MODEL, D_MODEL), D_MODEL ** -0.5),
    }


def reference(x_prompt, x_sample, cache_win_k, cache_win_v, state_hgrn, c_prompt, c_sample,
              norm_g, w_ada, b_ada, w_in, q_norm_g, k_norm_g, sinks, sg_norm_g, w_spatial, b_spatial,
              hgrn_lb, hgrn_norm_g, w_br_a, w_br_b, w_br_c, w_out):
    lower_bounds = _hgrn_lower_bounds(hgrn_lb)
    b_p = x_prompt.shape[0]
    zero_buf = jnp.zeros((b_p, WINDOW, N_KV_HEADS, HEAD_DIM), x_prompt.dtype)
    zero_state = jnp.zeros((b_p, HGRN_HEADS, HGRN_DK, HGRN_DK), state_hgrn.dtype)
    y_p, y_s = x_prompt, x_sample
    pk, pv, ps, sk, sv, ss, sgv = [], [], [], [], [], [], []
    for l in range(DEPTH):
        w = (norm_g[l], w_ada[l], b_ada[l], w_in[l], q_norm_g[l], k_norm_g[l], sinks[l], sg_norm_g[l],
             w_spatial[l], b_spatial[l], hgrn_norm_g[l], w_br_a[l], w_br_b[l], w_br_c[l], w_out[l])
        y_p, k_l, v_l, s_l, _ = _trunk_layer(y_p, c_prompt, 0, zero_buf, zero_buf, zero_state,
                                             lower_bounds[l], *w)
        pk.append(k_l)
        pv.append(v_l)
        ps.append(s_l)
        y_s, k_l, v_l, s_l, vb_l = _trunk_layer(y_s, c_sample, PAST_LEN, cache_win_k[l], cache_win_v[l],
                                                state_hgrn[l], lower_bounds[l], *w)
        sk.append(k_l)
        sv.append(v_l)
        ss.append(s_l)
        sgv.append(vb_l)
    return (y_p, y_s, jnp.stack(pk), jnp.stack(pv), jnp.stack(ps),
            jnp.stack(sk), jnp.stack(sv), jnp.stack(ss), jnp.stack(sgv))
```

```python
import numpy as np
import concourse.bass as bass
import concourse.mybir as mybir
from concourse.ap import AP
from concourse.bass_utils import run_bass_kernel_spmd

F32 = mybir.dt.float32
BF16 = mybir.dt.bfloat16
ALU = mybir.AluOpType
AF = mybir.ActivationFunctionType
AX = mybir.AxisListType

D = 1024
DEPTH = 2
SEQ = 4096
NTILES = SEQ // 128
PAST = 16384
HD = 64
NQH = 8
NKV = 2
WA = 512
DK = 128
HH = 4
EPS = 1e-6
TINY = float(np.finfo(np.float32).tiny)
THETA = 500000.0
IN_W = 7936
OFF = dict(qa=0, ka=512, va=640, ga=768, ub=1280, vb=1792, gb=2304, qc=2816, fc=3328, ic=3840,
           gc=4352, ma=4864, mb=5888, mc=6912)


class Clock:
    __slots__ = ("sem", "val", "name")

    def __init__(self, sem, name):
        self.sem = sem
        self.val = 0
        self.name = name


class Buf:
    __slots__ = ("w", "r", "name", "dead")

    def __init__(self, name=""):
        self.w = {}
        self.r = {}
        self.name = name
        self.dead = False


class _Rec:
    def __init__(self):
        self.call = None

    def __getattr__(self, name):
        def f(*a, **k):
            self.call = (name, a, k)
            return self
        return f


class Eng:
    def __init__(self, name, sem):
        self.name = name
        self.clock = Clock(sem, name)
        self.seen = {}
        self.prog = []
        self.nops = 0

    def _wait(self, clock, val):
        if val <= 0:
            return
        if clock is self.clock and (self.name == "pe" or val > self.clock.val):
            return
        if self.seen.get(clock, 0) >= val:
            return
        self.seen[clock] = val
        self.prog.append(("wait", clock, val))

    def deps(self, reads=(), writes=()):
        for b in reads:
            assert not b.dead, "stale ring handle read: " + b.name
            for c, v in b.w.items():
                self._wait(c, v)
        for b in writes:
            assert not b.dead, "stale ring handle write: " + b.name
            for c, v in b.w.items():
                self._wait(c, v)
            for c, v in b.r.items():
                self._wait(c, v)

    @staticmethod
    def _mark(clock, val, reads, writes, partial):
        for b in reads:
            if b.r.get(clock, 0) < val:
                b.r[clock] = val
        for b in writes:
            if not partial:
                b.w = {}
            b.r = {}
            b.w[clock] = val

    def op(self, fn, reads=(), writes=(), partial=False, inc=True):
        rec = _Rec()
        fn(rec)
        name, a, k = rec.call
        self.deps(reads, writes)
        self.nops += 1
        if inc:
            self.clock.val += 1
            self.prog.append(("op", name, a, k, self.clock.val))
            self._mark(self.clock, self.clock.val, reads, writes, partial)
        else:
            self.prog.append(("op", name, a, k, None))
            self._mark(self.clock, self.clock.val + 1, reads, writes, partial)

    def dma(self, dclock, out, in_, reads=(), writes=(), partial=False):
        self.deps(reads, writes)
        self._wait(dclock, dclock.val)
        dclock.val += 16
        self.prog.append(("dma", out, in_, dclock.sem))
        self._mark(dclock, dclock.val, reads, writes, partial)


class FW:
    def __init__(self, nc):
        self.nc = nc
        self.pe = Eng("pe", nc.alloc_semaphore("s_pe"))
        self.act = Eng("act", nc.alloc_semaphore("s_act"))
        self.dve = Eng("dve", nc.alloc_semaphore("s_dve"))
        self.pool = Eng("pool", nc.alloc_semaphore("s_pool"))
        self.sp = Eng("sp", nc.alloc_semaphore("s_sp"))
        self.engs = (self.pe, self.act, self.dve, self.pool, self.sp)
        self.dclocks = []
        self._n = 0

    def dclock(self, name):
        c = Clock(self.nc.alloc_semaphore("d_" + name), name)
        self.dclocks.append(c)
        return c

    def barrier(self):
        for e in self.engs:
            for o in self.engs:
                if o is not e:
                    e._wait(o.clock, o.clock.val)
            for c in self.dclocks:
                e._wait(c, c.val)

    def finish(self):
        for c in self.dclocks:
            self.sp._wait(c, c.val)
        for e in (self.pe, self.act, self.dve, self.pool):
            self.sp._wait(e.clock, e.clock.val)
        eclocks = {e.clock: e for e in self.engs}
        waited = {e.clock: set() for e in self.engs}
        for e in self.engs:
            for ent in e.prog:
                if ent[0] == "wait" and ent[1] in waited:
                    waited[ent[1]].add(ent[2])
        semval = {}
        for c, ws in waited.items():
            semval[c] = {v: i + 1 for i, v in enumerate(sorted(ws))}
        self.n_inc = {eclocks[c].name: len(ws) for c, ws in waited.items()}

        def emit(eng_obj, e):
            mine = semval[eng_obj.clock]
            for ent in eng_obj.prog:
                if ent[0] == "wait":
                    c, v = ent[1], ent[2]
                    e.wait_ge(c.sem, semval[c][v] if c in semval else v)
                elif ent[0] == "op":
                    ins = getattr(e, ent[1])(*ent[2], **ent[3])
                    if ent[4] is not None and ent[4] in mine:
                        ins.then_inc(eng_obj.clock.sem, 1)
                else:
                    e.dma_start(out=ent[1], in_=ent[2]).then_inc(ent[3], 16)
        with self.nc.Block() as block:
            @block.tensor
            def _(e):
                emit(self.pe, e)

            @block.scalar
            def _(e):
                emit(self.act, e)

            @block.vector
            def _(e):
                emit(self.dve, e)

            @block.gpsimd
            def _(e):
                emit(self.pool, e)

            @block.sync
            def _(e):
                emit(self.sp, e)


class Ring:
    def __init__(self, nc, name, n, shape, dtype, psum=False):
        self.items = []
        for i in range(n):
            if psum:
                t = nc.alloc_psum_tensor(f"{name}{i}", shape, dtype)
            else:
                t = nc.alloc_sbuf_tensor(f"{name}{i}", shape, dtype)
            self.items.append((t, Buf(f"{name}{i}")))
        self.i = 0

    def get(self):
        t, b = self.items[self.i]
        nb = Buf(b.name)
        nb.w = dict(b.w)
        nb.r = dict(b.r)
        b.dead = True
        self.items[self.i] = (t, nb)
        self.i = (self.i + 1) % len(self.items)
        return t, nb


def bcast(a, dims):
    return AP(a.tensor, a.offset, [list(a.ap[0])] + [list(d) for d in dims])


def _consts(nt):
    c = {}
    j = np.arange(128)
    c["ident"] = np.eye(128, dtype=np.float32)
    mp = (j[:, None] > j[None, :]).astype(np.float32)
    mc = (j[:, None] <= j[None, :]).astype(np.float32)
    c["amask_p"] = np.stack([mp, mc], 1).copy()
    bq = j // 8
    iq = j % 8
    ms = np.zeros((128, 17, 128), np.float32)
    for b in range(16):
        ms[:, b, :] = (bq[None, :] == b) & (j[:, None] > iq[None, :])
    ms[:, 16, :] = (bq[:, None] == bq[None, :]) & (iq[:, None] <= iq[None, :])
    c["amask_s"] = ms
    inv = (np.float32(THETA) ** (-np.arange(8, dtype=np.float32) / np.float32(8))).astype(np.float32)

    def rope(pos):
        ang = pos.astype(np.float32)[:, None] * inv[None, :]
        co = np.cos(ang).astype(np.float32)
        si = np.sin(ang).astype(np.float32)
        return np.concatenate([co, co], 1), np.concatenate([-si, si], 1)
    cs, sn = rope(np.arange(nt * 128))
    c["ropec_p"] = cs.reshape(nt, 128, 16).transpose(1, 0, 2).copy()
    c["ropes_p"] = sn.reshape(nt, 128, 16).transpose(1, 0, 2).copy()
    cs, sn = rope(PAST + iq)
    c["ropec_s"] = cs.reshape(128, 1, 16).copy()
    c["ropes_s"] = sn.reshape(128, 1, 16).copy()
    for kind, L in (("p", 16), ("s", 8)):
        ch = j // L
        loc = j % L
        ref = L // 2 - 1
        same = ch[:, None] == ch[None, :]
        le = loc[:, None] <= loc[None, :]
        MG = (same & le).astype(np.float32)
        MR = (same & (loc[:, None] <= ref)).astype(np.float32)
        ML = (same & (loc[:, None] > loc[None, :])).astype(np.float32)
        c["hm_" + kind] = np.stack([MG, MG - MR, ML], 1).copy()
        nc_ = 128 // L
        sel = (ch[:, None] == np.arange(nc_)[None, :]).astype(np.float32)
        c["csel_" + kind] = sel.copy()
        c["cmask_" + kind] = (same & le).astype(np.float32)
    c["smask_p"] = (j[:, None] <= j[None, :]).astype(np.float32)
    c["smask_s"] = ((bq[:, None] == bq[None, :]) & (iq[:, None] <= iq[None, :])).astype(np.float32)
    return c


def build_program(nt=NTILES, G=3, do_sample=True, debug=None, worder=None):
    nc = bass.Bass("TRN2", target_bir_lowering=False)
    fw = FW(nc)
    dbg_clk = []

    def dbg(name, ap_, buf, tl, l, kind):
        if debug is None or (kind, l) != debug or (kind == "p" and tl["ti"] != 0):
            return
        n = ap_.shape[-1] if len(ap_.shape) == 2 else int(np.prod(ap_.shape[1:]))
        dt_ = nc.dram_tensor("dbg_" + name, [ap_.shape[0], n], ap_.dtype, kind="ExternalOutput").ap()
        if not dbg_clk:
            dbg_clk.append(fw.dclock("dbg"))
        src = ap_ if len(ap_.shape) == 2 else ap_.rearrange("p a b -> p (a b)")
        fw.sp.dma(dbg_clk[0], dt_, src, reads=[buf])
    pe, act, dve, pool, sp = fw.pe, fw.act, fw.dve, fw.pool, fw.sp

    def din(name, shape):
        return nc.dram_tensor(name, list(shape), F32, kind="ExternalInput").ap()

    def dout(name, shape):
        return nc.dram_tensor(name, list(shape), F32, kind="ExternalOutput").ap()

    xp = din("xp", [nt * 128, D])
    xs = din("xs", [128, D])
    cp = din("cp", [128, D])
    cs_ = din("cs", [128, D])
    ckT = din("ckT", [DEPTH, 16, 2, 64, 128])
    ck = din("ck", [DEPTH, 16, 128, 128])
    cv = din("cv", [DEPTH, 16, 128, 128])
    sth = din("sth", [DEPTH, 16, 4, 128, 128])
    norm_g = din("norm_g", [DEPTH, D])
    w_ada = din("w_ada", [DEPTH, D, 3 * D])
    b_ada = din("b_ada", [DEPTH, 3 * D])
    w_in = din("w_in", [DEPTH, D, IN_W])
    q_norm_g = din("q_norm_g", [DEPTH, HD])
    k_norm_g = din("k_norm_g", [DEPTH, HD])
    sinks = din("sinks", [DEPTH, NQH])
    sg_norm_g = din("sg_norm_g", [DEPTH, 512])
    wsT_p = din("wsT_p", [DEPTH, 8, 128, 128])
    wsT_s = din("wsT_s", [DEPTH, 8, 128, 128])
    bsp_p = din("bsp_p", [DEPTH, 128, 8])
    bsp_s = din("bsp_s", [DEPTH, 128, 8])
    hgrn_lb = din("hgrn_lb", [DEPTH, 512])
    hgrn_norm_g = din("hgrn_norm_g", [DEPTH, 128])
    w_br = [din("w_br_a", [DEPTH, 512, D]), din("w_br_b", [DEPTH, 512, D]), din("w_br_c", [DEPTH, 512, D])]
    w_out = din("w_out", [DEPTH, D, D])
    cshape = {k: v.shape for k, v in _consts(1).items()}
    cshape["ropec_p"] = (128, nt, 16)
    cshape["ropes_p"] = (128, nt, 16)
    cd = {k: din("c_" + k, list(s)) for k, s in cshape.items()}

    yp = dout("yp", [nt * 128, D])
    ys = dout("ys", [128, D])
    wkp = dout("wkp", [DEPTH, 128, 128])
    wvp = dout("wvp", [DEPTH, 128, 128])
    hgp = dout("hgp", [DEPTH, 4, 128, 128])
    wks = dout("wks", [DEPTH, 16, 128, 128])
    wvs = dout("wvs", [DEPTH, 16, 128, 128])
    hgs = dout("hgs", [DEPTH, 16, 4, 128, 128])
    sgv = dout("sgv", [DEPTH, 128, 512])

    def sb(name, shape, dt=F32):
        return nc.alloc_sbuf_tensor(name, list(shape), dt)

    PS = Ring(nc, "ps", 6, [128, 512], F32, psum=True)
    PSO = Ring(nc, "pso", 2, [128, 512], F32, psum=True)
    SF = Ring(nc, "sf", 8, [128, 512], F32)
    SH = Ring(nc, "sh", 8, [128, 512], BF16)
    BIGF = Ring(nc, "bigf", 2, [128, D], F32)
    BIGH = Ring(nc, "bigh", 3, [128, D], BF16)
    SM = Ring(nc, "sm", 12, [128, 16], F32)
    NSLOT = 3
    WS = Ring(nc, "wslot", NSLOT, [128, 4096], BF16)
    wclk = [fw.dclock(f"w{i}") for i in range(NSLOT)]
    wclk_hw = [fw.dclock(f"wh{i}") for i in range(NSLOT)]

    ident = sb("ident", [128, 128], BF16)
    amask_p = sb("amask_p", [128, 2, 128], BF16)
    amask_s = sb("amask_s", [128, 17, 128], BF16)
    ropec_p = sb("ropec_p", [128, G, 16])
    ropes_p = sb("ropes_p", [128, G, 16])
    b_rope = Buf("rope")
    ropec_s = sb("ropec_s", [128, 1, 16])
    ropes_s = sb("ropes_s", [128, 1, 16])
    hm = {k: sb("hm_" + k, [128, 3, 128]) for k in "ps"}
    csel = {"p": sb("csel_p", [128, 8]), "s": sb("csel_s", [128, 16])}
    cselh = {"p": sb("cselh_p", [128, 8], BF16), "s": sb("cselh_s", [128, 16], BF16)}
    cmask = {k: sb("cmask_" + k, [128, 128], BF16) for k in "ps"}
    smask = {k: sb("smask_" + k, [128, 128], BF16) for k in "ps"}
    b_const = Buf("const")
    class RR:
        def __init__(self, name, n):
            self.c = [fw.dclock(f"{name}{i}") for i in range(n)]
            self.i = 0

        def get(self):
            self.i = (self.i + 1) % len(self.c)
            return self.c[self.i]
    cclk = RR("const", 6)
    cclk2 = RR("constp", 4)

    def cload(eng, clk, dst, src):
        eng.dma(clk.get(), dst, src, writes=[b_const], partial=True)

    cload(pool, cclk2, ident[:], cd["ident"])
    cload(pool, cclk2, amask_p[:], cd["amask_p"])
    cload(pool, cclk2, amask_s[:], cd["amask_s"])
    cload(sp, cclk, ropec_s[:], cd["ropec_s"])
    cload(sp, cclk, ropes_s[:], cd["ropes_s"])
    for k in "ps":
        cload(sp, cclk, hm[k][:], cd["hm_" + k])
        cload(sp, cclk, csel[k][:], cd["csel_" + k])
        cload(pool, cclk2, cselh[k][:], cd["csel_" + k])
        cload(pool, cclk2, cmask[k][:], cd["cmask_" + k])
        cload(pool, cclk2, smask[k][:], cd["smask_" + k])

    LC = []
    for l in range(DEPTH):
        lc = dict(
            qg=sb(f"qg{l}", [128, HD]), kg=sb(f"kg{l}", [128, HD]), sgg=sb(f"sgg{l}", [128, 512]),
            hgg=sb(f"hgg{l}", [128, 128]), esink=sb(f"esink{l}", [128, NQH]),
            bsp={"p": sb(f"bspp{l}", [128, 8]), "s": sb(f"bsps{l}", [128, 8])},
            wsT=sb(f"wsT{l}", [128, 8, 128], BF16),
            A=sb(f"modA{l}", [128, D]), shift=sb(f"modS{l}", [128, D]), gate=sb(f"modG{l}", [128, D]),
            b_mod=Buf(f"mod{l}"),
        )
        cload(sp, cclk, lc["qg"][:], q_norm_g[l].partition_broadcast(128))
        cload(sp, cclk, lc["kg"][:], k_norm_g[l].partition_broadcast(128))
        cload(sp, cclk, lc["sgg"][:], sg_norm_g[l].partition_broadcast(128))
        cload(sp, cclk, lc["hgg"][:], hgrn_norm_g[l].partition_broadcast(128))
        cload(sp, cclk, lc["esink"][:], sinks[l].partition_broadcast(128))
        cload(sp, cclk, lc["bsp"]["p"][:], bsp_p[l])
        cload(sp, cclk, lc["bsp"]["s"][:], bsp_s[l])
        cload(pool, cclk2, lc["wsT"][:], wsT_p[l].rearrange("g s t -> s g t"))
        LC.append(lc)
    lbt = sb("lbt", [128, 512])
    oml = sb("oml", [128, 512])
    lb0, _lb0b = SF.get()
    cload(sp, cclk, lb0[:], hgrn_lb[0].partition_broadcast(128))
    cload(sp, cclk, lbt[:], hgrn_lb[1].partition_broadcast(128))
    dve.op(lambda e: e.tensor_tensor(out=lbt[:], in0=lbt[:], in1=lb0[:], op=ALU.subtract), reads=[b_const], writes=[b_const], partial=True)
    act.op(lambda e: e.activation(out=lbt[:], in_=lbt[:], func=AF.Sigmoid), reads=[b_const], writes=[b_const], partial=True)
    dve.op(lambda e: e.tensor_scalar(out=oml[:], in0=lbt[:], scalar1=-1.0, scalar2=1.0, op0=ALU.mult, op1=ALU.add),
           reads=[b_const], writes=[b_const], partial=True)
    for l in range(DEPTH):
        lc = LC[l]
        act.op(lambda e, lc=lc: e.activation(out=lc["esink"][:], in_=lc["esink"][:], func=AF.Exp),
               reads=[b_const], writes=[b_const], partial=True)
        dve.op(lambda e, lc=lc: e.tensor_tensor(out=lc["wsT"][:], in0=lc["wsT"][:],
                                                in1=bcast(smask["p"][:], [[0, 8], [1, 128]]), op=ALU.mult),
               reads=[b_const], writes=[b_const], partial=True)

    S_p = [sb(f"S_p{l}", [128, 4, 128]) for l in range(DEPTH)]
    Sbf_p = [[sb(f"Sbf_p{l}_{q}", [128, 4, 128], BF16) for q in range(2)] for l in range(DEPTH)]
    b_Sp = [[Buf(f"S_p{l}_{h}") for h in range(4)] for l in range(DEPTH)]
    b_Sbfp = [[[Buf(f"Sbf_p{l}_{q}_{h}") for h in range(4)] for q in range(2)] for l in range(DEPTH)]
    for l in range(DEPTH):
        dve.op(lambda e, l=l: e.memset(S_p[l][:], 0.0), writes=b_Sp[l])
        for q in range(2):
            dve.op(lambda e, l=l, q=q: e.memset(Sbf_p[l][q][:], 0.0), writes=b_Sbfp[l][q])
    NKS = G + 1
    kT_sl = [[sb(f"kT{l}_{i}", [128, 128], BF16) for i in range(NKS)] for l in range(DEPTH)]
    va_sl = [[sb(f"va{l}_{i}", [128, 2, 65], BF16) for i in range(NKS)] for l in range(DEPTH)]
    b_kv = [[Buf(f"kv{l}_{i}") for i in range(NKS)] for l in range(DEPTH)]
    for l in range(DEPTH):
        for i in range(NKS):
            dve.op(lambda e, l=l, i=i: e.memset(va_sl[l][i][:], 1.0), writes=[b_kv[l][i]])
    QgT_m = sb("QgT_m", [128, 4, 8, 128], BF16)
    KtT_m = sb("KtT_m", [128, 4, 8, 128], BF16)
    Kd_m = sb("Kd_m", [128, 8, 512], BF16)
    QtT = sb("QtT", [128, 4, 128], BF16)
    aT_sb = sb("aT_sb", [128, 4, 128], BF16)
    egl_all = sb("egl", [128, G, 4, 16])
    b_QgT, b_KtT, b_Kd, b_QtT, b_aT, b_egl = [Buf(n) for n in ("QgT", "KtT", "Kdm", "QtT", "aT", "egl")]
    dve.op(lambda e: e.memset(QgT_m[:], 0.0), writes=[b_QgT])
    dve.op(lambda e: e.memset(KtT_m[:], 0.0), writes=[b_KtT])

    ARENA_TILE = 19 * 1024
    arena_bytes = max(G * ARENA_TILE, (ARENA_TILE + 31 * 1024) if do_sample else 0)
    arena = sb("arena", [128, arena_bytes // 2], BF16)

    class Carver:
        def __init__(self, pos):
            self.pos = pos

        def take(self, shape, dt=F32):
            n = int(np.prod(shape[1:])) * (4 if dt == F32 else 2)
            assert self.pos + n <= arena_bytes, "arena overflow"
            v = arena[0:shape[0], self.pos // 2:(self.pos + n) // 2]
            if dt == F32:
                v = v.bitcast(F32)
            if len(shape) == 3:
                v = v.rearrange("p (a b) -> p a b", a=shape[1])
            elif len(shape) == 4:
                v = v.rearrange("p (a b c) -> p a b c", a=shape[1], b=shape[2])
            self.pos += (n + 31) // 32 * 32
            return v
    TB = []
    for i in range(G):
        cv_ = Carver(i * ARENA_TILE)
        TB.append(dict(
            x=cv_.take([128, D]), hT=cv_.take([128, 8, 128], BF16), yT=cv_.take([128, 12, 128], BF16),
            qT=cv_.take([128, 4, 128], BF16), sA=cv_.take([128, 512], BF16), sB=cv_.take([128, 512], BF16),
            sC=cv_.take([128, 512], BF16), sgm=cv_.take([128, D], BF16), mg=cv_.take([128, D]),
            b={k: Buf(f"{k}{i}") for k in ("x", "hT", "yT", "qT", "sA", "sB", "sC", "sgm", "mg")},
        ))
    xclk = [fw.dclock(f"x{i}") for i in range(G)]
    yclk = [fw.dclock(f"y{i}") for i in range(G)]
    oclk = RR("osm", 8)
    if do_sample:
        sclk = [fw.dclock(f"smp{i}") for i in range(4)]

    wq = []
    wstate = {"issued": 0, "used": 0}

    def wdesc(kind, l, a=0, n=512):
        if kind == "in":
            return (w_in[l][:, a:a + n].rearrange("(c p) n -> p c n", p=128), 8, n)
        if kind == "ada":
            return (w_ada[l][:, a:a + n].rearrange("(c p) n -> p c n", p=128), 8, n)
        if kind == "br":
            return (w_br[a][l].rearrange("(c p) n -> p c n", p=128), 4, 1024)
        if kind == "out":
            return (w_out[l][:, a:a + n].rearrange("(c p) n -> p c n", p=128), 8, n)
        raise ValueError(kind)

    worder_out = []
    wq_desc = []
    if worder is not None:
        for d_ in worder:
            wq.append(wdesc(*d_))
            wq_desc.append(d_)
    wscr = {}
    wbclk = RR("wb", 4)

    def w_issue():
        i = wstate["issued"]
        if i >= len(wq):
            return
        src, kc, n = wq[i]
        desc = wq_desc[i]
        slot = i % NSLOT
        t, b = WS.items[slot]
        flat = t[:, 0:kc * n]
        if desc in wscr:
            scr, scrb = wscr[desc]
            sp.dma(wclk_hw[slot], flat, scr, reads=[scrb], writes=[b])
        else:
            dst = flat.rearrange("p (c n) -> p c n", c=kc)
            pool.dma(wclk[slot], dst, src, writes=[b])
            scr = nc.dram_tensor("wscr_%d" % len(wscr), [128, kc * n], BF16).ap()
            scrb = Buf("wscr")
            wscr[desc] = (scr, scrb)
            sp.dma(wbclk.get(), scr, flat, reads=[b], writes=[scrb])
        wstate["issued"] += 1

    def w_next(desc):
        i = wstate["used"]
        worder_out.append(desc)
        if worder is None:
            wq.append(wdesc(*desc))
            wq_desc.append(desc)
        else:
            assert worder[i] == desc, (i, worder[i], desc)
        while wstate["issued"] < min(len(wq), i + NSLOT):
            w_issue()
        src, kc, n = wq[i]
        t, b = WS.items[i % NSLOT]
        wstate["used"] += 1
        return t[:, 0:kc * n].rearrange("p (c n) -> p c n", c=kc), b

    def rr(gens):
        gens = list(gens)
        while gens:
            nxt = []
            for g_ in gens:
                try:
                    next(g_)
                    nxt.append(g_)
                except StopIteration:
                    pass
            gens = nxt

    def proj(lhsT_t, lhsT_b, kc, wv, wb, c0, n):
        pt, pb = PS.get()
        for c in range(kc):
            pe.op(lambda e, c=c: e.matmul(pt[:, 0:n], lhsT=lhsT_t[:, c, :], rhs=wv[:, c, c0:c0 + n],
                                          start=(c == 0), stop=(c == kc - 1)),
                  reads=[lhsT_b, wb], writes=[pb], partial=(c > 0), inc=(c == kc - 1))
        return pt, pb

    def transposes(src_t, src_b, nblk, dst_t, dst_b, dst_blk0, full_overwrite=False):
        for j0 in range(0, nblk, 4):
            nb = min(4, nblk - j0)
            pt, pb = PS.get()
            for j in range(nb):
                pe.op(lambda e, j=j, j0=j0: e.matmul(pt[:, j * 128:(j + 1) * 128],
                                                      lhsT=src_t[:, (j0 + j) * 128:(j0 + j + 1) * 128],
                                                      rhs=ident[:], start=True, stop=True),
                      reads=[src_b, b_const], writes=[pb], partial=(j > 0), inc=(j == nb - 1))
            eng = act if (j0 // 4) % 2 == 0 else dve
            dview = dst_t[:, dst_blk0 + j0:dst_blk0 + j0 + nb, :]
            sview = pt[:, 0:nb * 128].rearrange("p (j t) -> p j t", j=nb)
            if eng is act:
                act.op(lambda e, dview=dview, sview=sview: e.copy(out=dview, in_=sview),
                       reads=[pb], writes=[dst_b], partial=not (full_overwrite and j0 == 0 and nb == nblk))
            else:
                dve.op(lambda e, dview=dview, sview=sview: e.tensor_copy(out=dview, in_=sview),
                       reads=[pb], writes=[dst_b], partial=True)

    def rstd_from_ss(ss_t, ss_b, n, width, denom):
        v = ss_t[:, 0:n]
        dve.op(lambda e: e.tensor_scalar(out=v, in0=v, scalar1=1.0 / denom, scalar2=EPS, op0=ALU.mult, op1=ALU.add),
               reads=[ss_b], writes=[ss_b])
        act.op(lambda e: e.activation(out=v, in_=v, func=AF.Ln), reads=[ss_b], writes=[ss_b])
        act.op(lambda e: e.activation(out=v, in_=v, func=AF.Exp, scale=-0.5), reads=[ss_b], writes=[ss_b])

    def headnorm_rope(pt, pb, c0, nh, gain_t, ropec, ropes):
        n = nh * 64
        sq, sqb = SF.get()
        act.op(lambda e: e.activation(out=sq[:, 0:n], in_=pt[:, c0:c0 + n], func=AF.Square), reads=[pb], writes=[sqb])
        ss, ssb = SM.get()
        dve.op(lambda e: e.tensor_reduce(out=ss[:, 0:nh], in_=sq[:, 0:n].rearrange("p (h d) -> p h d", h=nh),
                                         axis=AX.X, op=ALU.add), reads=[sqb], writes=[ssb])
        rstd_from_ss(ss, ssb, nh, n, HD)
        qn, qb = SF.get()
        q3 = qn[:, 0:n].rearrange("p (h d) -> p h d", h=nh)
        dve.op(lambda e: e.tensor_tensor(out=q3, in0=pt[:, c0:c0 + n].rearrange("p (h d) -> p h d", h=nh),
                                         in1=bcast(ss[:, 0:nh], [[1, nh], [0, HD]]), op=ALU.mult),
               reads=[pb, ssb], writes=[qb])
        dve.op(lambda e: e.tensor_tensor(out=q3, in0=q3, in1=bcast(gain_t[:], [[0, nh], [1, HD]]), op=ALU.mult),
               reads=[qb, b_const], writes=[qb])
        tr, trb = SF.get()
        t3 = tr[:, 0:nh * 16].rearrange("p (h d) -> p h d", h=nh)
        dve.op(lambda e: e.tensor_tensor(out=t3[:, :, 0:8], in0=q3[:, :, 8:16], in1=bcast(ropes[:, 0:8], [[0, nh], [1, 8]]),
                                         op=ALU.mult), reads=[qb, b_const, b_rope], writes=[trb])
        dve.op(lambda e: e.tensor_tensor(out=t3[:, :, 8:16], in0=q3[:, :, 0:8], in1=bcast(ropes[:, 8:16], [[0, nh], [1, 8]]),
                                         op=ALU.mult), reads=[qb, b_const, b_rope], writes=[trb], partial=True)
        dve.op(lambda e: e.tensor_tensor(out=q3[:, :, 0:16], in0=q3[:, :, 0:16], in1=bcast(ropec[:, 0:16], [[0, nh], [1, 16]]),
                                         op=ALU.mult), reads=[qb, trb, b_const, b_rope], writes=[qb])
        dve.op(lambda e: e.tensor_tensor(out=q3[:, :, 0:16], in0=q3[:, :, 0:16], in1=t3, op=ALU.add),
               reads=[qb, trb], writes=[qb])
        return qn, qb

    def compute_mods(c_dram):
        ct, ctb = BIGF.get()
        sp.dma(cclk.get(), ct[:], c_dram, writes=[ctb])
        ch, chb = BIGH.get()
        act.op(lambda e: e.activation(out=ch[:], in_=ct[:], func=AF.Silu), reads=[ctb], writes=[chb])
        cT = TB[0]["hT"]
        cTb = TB[0]["b"]["hT"]
        transposes(ch, chb, 8, cT, cTb, 0, full_overwrite=True)
        for l in range(DEPTH):
            lc = LC[l]
            for jb in range(6):
                wv, wb = w_next(("ada", l, jb * 512, 512))
                pt, pb = proj(cT, cTb, 8, wv, wb, 0, 512)
                bt, bb = SF.get()
                sp.dma(cclk.get(), bt[:], b_ada[l, jb * 512:(jb + 1) * 512].partition_broadcast(128), writes=[bb])
                half = (jb % 2) * 512
                if jb < 2:
                    dst = lc["shift"][:, half:half + 512]
                    dve.op(lambda e, dst=dst, pt=pt, bt=bt: e.tensor_tensor(out=dst, in0=pt[:], in1=bt[:], op=ALU.add),
                           reads=[pb, bb], writes=[lc["b_mod"]], partial=True)
                elif jb < 4:
                    gt, gb_ = SF.get()
                    sp.dma(cclk.get(), gt[:], norm_g[l, half:half + 512].partition_broadcast(128), writes=[gb_])
                    tt, tb_ = SF.get()
                    dve.op(lambda e, pt=pt, bt=bt, tt=tt: e.tensor_tensor(out=tt[:], in0=pt[:], in1=bt[:], op=ALU.add),
                           reads=[pb, bb], writes=[tb_])
                    dst = lc["A"][:, half:half + 512]
                    dve.op(lambda e, dst=dst, tt=tt, gt=gt: e.scalar_tensor_tensor(out=dst, in0=tt[:], scalar=1.0, in1=gt[:],
                                                                                  op0=ALU.add, op1=ALU.mult),
                           reads=[tb_, gb_], writes=[lc["b_mod"]], partial=True)
                else:
                    dst = lc["gate"][:, half:half + 512]
                    dve.op(lambda e, dst=dst, pt=pt, bt=bt: e.tensor_tensor(out=dst, in0=pt[:], in1=bt[:], op=ALU.add),
                           reads=[pb, bb], writes=[lc["b_mod"]], partial=True)

    def mods_blocks():
        for l in range(DEPTH):
            for jb in range(6):
                wq.append(wdesc("ada", l, jb * 512, 512))

    LAYER_BLOCKS = (["qa", "ga", "kv", "ub", "gb", "vb", "qc", "ic", "gc", "fc"]
                    + ["ma0", "ma1", "bra", "mb0", "mb1", "brb", "mc0", "mc1", "brc", "out0", "out1"])

    def layer_blocks(l):
        for name in LAYER_BLOCKS:
            if name == "kv":
                wq.append(wdesc("in", l, OFF["ka"], 256))
            elif name[:2] in ("ma", "mb", "mc"):
                wq.append(wdesc("in", l, OFF[name[:2]] + 512 * int(name[2]), 512))
            elif name.startswith("br"):
                wq.append(wdesc("br", l, "abc".index(name[2])))
            elif name.startswith("out"):
                wq.append(wdesc("out", l, 512 * int(name[3]), 512))
            else:
                wq.append(wdesc("in", l, OFF[name], 512))

    fw.marks = []

    def mark(label):
        fw.marks.append((label, pe.nops))

    def run_group(tiles, kind):
        NC = 8 if kind == "p" else 16
        L = 128 // NC
        if kind == "p":
            t0_, ng_ = tiles[0]["ti"], len(tiles)
            sp.dma(cclk.get(), ropec_p[:, 0:ng_, :], cd["ropec_p"][:, t0_:t0_ + ng_, :], writes=[b_rope])
            sp.dma(cclk.get(), ropes_p[:, 0:ng_, :], cd["ropes_p"][:, t0_:t0_ + ng_, :], writes=[b_rope], partial=True)
        for l in range(DEPTH):
            lc = LC[l]
            last_layer = (l == DEPTH - 1)
            if kind == "s":
                pool.dma(sclk[0], kTc[:], ckT[l].rearrange("b g d j -> (g d) b j"), writes=[b_kTc])
                for g_ in range(2):
                    pool.dma(sclk[1], Vc[:, :, g_, 0:64], cv[l][:, :, 64 * g_:64 * g_ + 64].rearrange("b j d -> j b d"),
                             writes=[b_Vc], partial=True)
                sp.dma(oclk.get(), wks[l][:, 0:120, :], ck[l][:, 8:128, :])
                sp.dma(oclk.get(), wvs[l][:, 0:120, :], cv[l][:, 8:128, :])
            mark("S0")
            def s0_body(tl):
                tb = TB[tl["slot"]]
                B = tb["b"]
                if l == 0:
                    src = xs if kind == "s" else xp[tl["ti"] * 128:(tl["ti"] + 1) * 128, :]
                    sp.dma(xclk[tl["slot"]], tb["x"][:], src, writes=[B["x"]])
                ss, ssb = SM.get()
                hb, hbb = BIGH.get()
                act.op(lambda e, tb=tb, ss=ss, hb=hb: e.activation(out=hb[:], in_=tb["x"][:], func=AF.Square, accum_out=ss[:, 0:1]),
                       reads=[B["x"]], writes=[ssb, hbb])
                rstd_from_ss(ss, ssb, 1, D, D)
                t1, t1b = BIGF.get()
                dve.op(lambda e, tb=tb, ss=ss, t1=t1: e.scalar_tensor_tensor(out=t1[:], in0=tb["x"][:], scalar=ss[:, 0:1],
                                                                            in1=lc["A"][:], op0=ALU.mult, op1=ALU.mult),
                       reads=[B["x"], ssb, lc["b_mod"]], writes=[t1b])
                dve.op(lambda e, t1=t1, hb=hb: e.tensor_tensor(out=hb[:], in0=t1[:], in1=lc["shift"][:], op=ALU.add),
                       reads=[t1b, lc["b_mod"]], writes=[hbb])
                yield
                dbg("h", hb[:], hbb, tl, l, kind)
                transposes(hb, hbb, 8, tb["hT"], B["hT"], 0, full_overwrite=True)
            rr([s0_body(tl) for tl in tiles])

            mark("qa")
            wv, wb = w_next(("in", l, OFF["qa"], 512))

            def qa_body(tl):
                tb = TB[tl["slot"]]
                B = tb["b"]
                rc = ropec_s[:, 0, :] if kind == "s" else ropec_p[:, tl["slot"], :]
                rs = ropes_s[:, 0, :] if kind == "s" else ropes_p[:, tl["slot"], :]
                pt, pb = proj(tb["hT"], B["hT"], 8, wv, wb, 0, 512)
                yield
                qn, qb = headnorm_rope(pt, pb, 0, 8, lc["qg"], rc, rs)
                dbg("q", qn[:], qb, tl, l, kind)
                qh, qhb = SH.get()
                dve.op(lambda e, qh=qh, qn=qn: e.tensor_copy(
                    out=qh[:].rearrange("p (j g d) -> p g j d", j=4, g=2),
                    in_=qn[:].rearrange("p (g j d) -> p g j d", g=2, j=4)), reads=[qb], writes=[qhb])
                yield
                transposes(qh, qhb, 4, tb["qT"], B["qT"], 0, full_overwrite=True)
            rr([qa_body(tl) for tl in tiles])
            mark("ga")
            wv, wb = w_next(("in", l, OFF["ga"], 512))

            def ga_body(tl):
                tb = TB[tl["slot"]]
                B = tb["b"]
                pt, pb = proj(tb["hT"], B["hT"], 8, wv, wb, 0, 512)
                yield
                act.op(lambda e, tb=tb, pt=pt: e.activation(out=tb["sA"][:], in_=pt[:], func=AF.Silu), reads=[pb], writes=[B["sA"]])
            rr([ga_body(tl) for tl in tiles])
            mark("kv")
            wv, wb = w_next(("in", l, OFF["ka"], 256))

            def kv_body(idx, tl):
                tb = TB[tl["slot"]]
                B = tb["b"]
                rc = ropec_s[:, 0, :] if kind == "s" else ropec_p[:, tl["slot"], :]
                rs = ropes_s[:, 0, :] if kind == "s" else ropes_p[:, tl["slot"], :]
                pt, pb = proj(tb["hT"], B["hT"], 8, wv, wb, 0, 256)
                yield
                kn, knb = headnorm_rope(pt, pb, 0, 2, lc["kg"], rc, rs)
                cur = idx + 1
                kT_c, va_c, bkv_c = kT_sl[l][cur], va_sl[l][cur], b_kv[l][cur]
                kh, khb = SH.get()
                dve.op(lambda e, kh=kh, kn=kn: e.tensor_copy(out=kh[:, 0:128], in_=kn[:, 0:128]), reads=[knb], writes=[khb])
                dve.op(lambda e, pt=pt, va_c=va_c: e.tensor_copy(out=va_c[:, :, 0:64],
                                                                 in_=pt[:, 128:256].rearrange("p (g d) -> p g d", g=2)),
                       reads=[pb], writes=[bkv_c], partial=True)
                want_out = (kind == "s") or (tl["ti"] == nt - 1)
                if want_out:
                    vf, vfb = SF.get()
                    act.op(lambda e, vf=vf, pt=pt: e.copy(out=vf[:, 0:128], in_=pt[:, 128:256]), reads=[pb], writes=[vfb])
                yield
                p2, p2b = PS.get()
                pe.op(lambda e, p2=p2, kh=kh: e.matmul(p2[:, 0:128], lhsT=kh[:, 0:128], rhs=ident[:], start=True, stop=True),
                      reads=[khb, b_const], writes=[p2b])
                act.op(lambda e, p2=p2, kT_c=kT_c: e.copy(out=kT_c[:], in_=p2[:, 0:128]), reads=[p2b], writes=[bkv_c], partial=True)
                yield
                if want_out:
                    if kind == "p":
                        sp.dma(oclk.get(), wkp[l], kn[:, 0:128], reads=[knb])
                        sp.dma(oclk.get(), wvp[l], vf[:, 0:128], reads=[vfb])
                    else:
                        for b in range(16):
                            sp.dma(oclk.get(), wks[l][b, 120:128, :], kn[8 * b:8 * b + 8, 0:128], reads=[knb])
                            sp.dma(oclk.get(), wvs[l][b, 120:128, :], vf[8 * b:8 * b + 8, 0:128], reads=[vfb])
                ya, yab = SF.get()
                pairs = []
                accs = []
                for g in range(2):
                    if kind == "p":
                        kts = []
                        if tl["ti"] > 0:
                            kts.append((kT_sl[l][cur - 1], va_sl[l][cur - 1][:, g, :], b_kv[l][cur - 1], amask_p[:, 0, :]))
                        kts.append((kT_c, va_c[:, g, :], bkv_c, amask_p[:, 1, :]))
                        kts = [(k_[64 * g:64 * g + 64, :], v_, [b_], m_) for (k_, v_, b_, m_) in kts]
                    else:
                        kts = [(kTc[64 * g:64 * g + 64, b, :], Vc[:, b, g, :], [b_kTc, b_Vc], amask_s[:, b, :]) for b in range(16)]
                        kts.append((kT_c[64 * g:64 * g + 64, :], va_c[:, g, :], [bkv_c], amask_s[:, 16, :]))
                    ot, ob = PSO.get()
                    o3 = ot[:, 0:260].rearrange("p (j d) -> p j d", j=4)
                    accs.append((ot, ob, o3))
                    for ki, kt in enumerate(kts):
                        pairs.append((g, ki, len(kts), kt))
                q_b = B["qT"]
                for p0 in range(0, len(pairs), 4):
                    batch = pairs[p0:p0 + 4]
                    work = []
                    for (g, ki, nk, (k_ap, v_ap, kvbs, m_ap)) in batch:
                        sc, scb = PS.get()
                        q_ap = tb["qT"][64 * g:64 * g + 64, :, :]
                        pe.op(lambda e, sc=sc, k_ap=k_ap, q_ap=q_ap: e.matmul(sc[:], lhsT=k_ap, rhs=q_ap, start=True, stop=True),
                              reads=[q_b] + kvbs, writes=[scb])
                        work.append((sc, scb))
                    pts = []
                    for (sc, scb) in work:
                        ptt, ptb = SH.get()
                        act.op(lambda e, sc=sc, ptt=ptt: e.activation(out=ptt[:], in_=sc[:], func=AF.Exp, scale=HD ** -0.5),
                               reads=[scb], writes=[ptb])
                        pts.append((ptt, ptb))
                    for (ptt, ptb), (g, ki, nk, (k_ap, v_ap, kvbs, m_ap)) in zip(pts, batch):
                        dve.op(lambda e, ptt=ptt, m_ap=m_ap: e.tensor_tensor(
                            out=ptt[:].rearrange("p (j q) -> p j q", j=4), in0=ptt[:].rearrange("p (j q) -> p j q", j=4),
                            in1=bcast(m_ap, [[0, 4], [1, 128]]), op=ALU.mult), reads=[ptb, b_const], writes=[ptb])
                    for (ptt, ptb), (g, ki, nk, (k_ap, v_ap, kvbs, m_ap)) in zip(pts, batch):
                        ot, ob, o3 = accs[g]
                        for j in range(4):
                            pe.op(lambda e, j=j, o3=o3, ptt=ptt, v_ap=v_ap, ki=ki, nk=nk: e.matmul(
                                o3[:, j, :], lhsT=ptt[:, j * 128:(j + 1) * 128], rhs=v_ap,
                                start=(ki == 0 and j == 0), stop=(ki == nk - 1), skip_group_check=True),
                                reads=[ptb] + kvbs, writes=[ob], partial=not (ki == 0 and j == 0), inc=(j == 3))
                for g in range(2):
                    ot, ob, o3 = accs[g]
                    den, denb = SM.get()
                    dve.op(lambda e, den=den, o3=o3, g=g: e.tensor_tensor(out=den[:, 0:4], in0=o3[:, :, 64],
                                                                          in1=lc["esink"][:, 4 * g:4 * g + 4], op=ALU.add),
                           reads=[ob, b_const], writes=[denb])
                    dve.op(lambda e, den=den: e.reciprocal(out=den[:, 0:4], in_=den[:, 0:4]), reads=[denb], writes=[denb])
                    dve.op(lambda e, ya=ya, o3=o3, den=den, g=g: e.tensor_tensor(
                        out=ya[:, 256 * g:256 * g + 256].rearrange("p (j d) -> p j d", j=4), in0=o3[:, :, 0:64],
                        in1=bcast(den[:, 0:4], [[1, 4], [0, 64]]), op=ALU.mult),
                        reads=[ob, denb], writes=[yab], partial=(g == 1))
                dbg("k", kn[:, 0:128], knb, tl, l, kind)
                dbg("ya", ya[:], yab, tl, l, kind)
                yield
                yg, ygb = SH.get()
                dve.op(lambda e, yg=yg, ya=ya, tb=tb: e.tensor_tensor(out=yg[:], in0=ya[:], in1=tb["sA"][:], op=ALU.mult),
                       reads=[yab, B["sA"]], writes=[ygb])
                transposes(yg, ygb, 4, tb["yT"], B["yT"], 0)
            rr([kv_body(i_, tl) for i_, tl in enumerate(tiles)])
            if kind == "p":
                last = len(tiles)
                act.op(lambda e: e.copy(out=kT_sl[l][0][:], in_=kT_sl[l][last][:]), reads=[b_kv[l][last]], writes=[b_kv[l][0]], partial=True)
                dve.op(lambda e: e.tensor_copy(out=va_sl[l][0][:], in_=va_sl[l][last][:]), reads=[b_kv[l][last]], writes=[b_kv[l][0]], partial=True)
            mark("ub")
            wv, wb = w_next(("in", l, OFF["ub"], 512))

            def ub_body(tl):
                tb = TB[tl["slot"]]
                B = tb["b"]
                pt, pb = proj(tb["hT"], B["hT"], 8, wv, wb, 0, 512)
                yield
                act.op(lambda e, tb=tb, pt=pt: e.copy(out=tb["sB"][:], in_=pt[:]), reads=[pb], writes=[B["sB"]])
            rr([ub_body(tl) for tl in tiles])
            mark("gb")
            wv, wb = w_next(("in", l, OFF["gb"], 512))

            def gb_body(tl):
                tb = TB[tl["slot"]]
                B = tb["b"]
                pt, pb = proj(tb["hT"], B["hT"], 8, wv, wb, 0, 512)
                yield
                act.op(lambda e, tb=tb, pt=pt: e.activation(out=tb["sC"][:], in_=pt[:], func=AF.Silu), reads=[pb], writes=[B["sC"]])
                pool.op(lambda e, tb=tb: e.tensor_tensor(out=tb["sC"][:], in0=tb["sC"][:], in1=tb["sB"][:], op=ALU.mult),
                        reads=[B["sC"], B["sB"]], writes=[B["sC"]])
            rr([gb_body(tl) for tl in tiles])
            mark("vb")
            wv, wb = w_next(("in", l, OFF["vb"], 512))

            def vb_body(tl):
                tb = TB[tl["slot"]]
                B = tb["b"]
                pt, pb = proj(tb["hT"], B["hT"], 8, wv, wb, 0, 512)
                yield
                ss, ssb = SM.get()
                vh, vhb = SH.get()
                act.op(lambda e, pt=pt, ss=ss, vh=vh: e.activation(out=vh[:], in_=pt[:], func=AF.Square, accum_out=ss[:, 0:1]),
                       reads=[pb], writes=[ssb, vhb])
                rstd_from_ss(ss, ssb, 1, 512, 512)
                vn, vnb = SF.get()
                dve.op(lambda e, vn=vn, pt=pt, ss=ss: e.scalar_tensor_tensor(out=vn[:], in0=pt[:], scalar=ss[:, 0:1], in1=lc["sgg"][:],
                                                                            op0=ALU.mult, op1=ALU.mult),
                       reads=[pb, ssb, b_const], writes=[vnb])
                if kind == "s":
                    sp.dma(oclk.get(), sgv[l], vn[:], reads=[vnb])
                act.op(lambda e, vh=vh, vn=vn: e.copy(out=vh[:], in_=vn[:]), reads=[vnb], writes=[vhb])
                yield
                zt, zb = PS.get()
                for g8 in range(8):
                    pe.op(lambda e, g8=g8, zt=zt, vh=vh: e.matmul(zt[:, g8 * 64:(g8 + 1) * 64], lhsT=lc["wsT"][:, g8, :],
                                                                  rhs=vh[:, g8 * 64:(g8 + 1) * 64], start=True, stop=True),
                          reads=[vhb, b_const], writes=[zb], partial=(g8 > 0), inc=(g8 == 7))
                yield
                yb_, ybb = SF.get()
                dve.op(lambda e, yb_=yb_, zt=zt: e.tensor_tensor(out=yb_[:].rearrange("p (g d) -> p g d", g=8),
                                                                 in0=zt[:].rearrange("p (g d) -> p g d", g=8),
                                                                 in1=bcast(lc["bsp"][kind][:], [[1, 8], [0, 64]]), op=ALU.add),
                       reads=[zb, b_const], writes=[ybb])
                yg, ygb = SH.get()
                dve.op(lambda e, yg=yg, yb_=yb_, tb=tb: e.tensor_tensor(out=yg[:], in0=yb_[:], in1=tb["sC"][:], op=ALU.mult),
                       reads=[ybb, B["sC"]], writes=[ygb])
                dbg("ybg", yg[:], ygb, tl, l, kind)
                yield
                transposes(yg, ygb, 4, tb["yT"], B["yT"], 4)
            rr([vb_body(tl) for tl in tiles])
            mark("qcicgc")
            for nm in ("qc", "ic", "gc"):
                wv, wb = w_next(("in", l, OFF[nm], 512))

                def c3_body(tl, nm=nm, wv=wv, wb=wb):
                    tb = TB[tl["slot"]]
                    B = tb["b"]
                    pt, pb = proj(tb["hT"], B["hT"], 8, wv, wb, 0, 512)
                    yield
                    if nm == "qc":
                        act.op(lambda e, tb=tb, pt=pt: e.mul(out=tb["sA"][:], in_=pt[:], mul=DK ** -0.5), reads=[pb], writes=[B["sA"]])
                    elif nm == "ic":
                        act.op(lambda e, tb=tb, pt=pt: e.copy(out=tb["sB"][:], in_=pt[:]), reads=[pb], writes=[B["sB"]])
                    else:
                        act.op(lambda e, tb=tb, pt=pt: e.activation(out=tb["sC"][:], in_=pt[:], func=AF.Silu),
                               reads=[pb], writes=[B["sC"]])
                        pool.op(lambda e, tb=tb: e.tensor_tensor(out=tb["sC"][:].rearrange("p (h d) -> p h d", h=4),
                                                                 in0=tb["sC"][:].rearrange("p (h d) -> p h d", h=4),
                                                                 in1=bcast(lc["hgg"][:], [[0, 4], [1, 128]]), op=ALU.mult),
                                reads=[B["sC"], b_const], writes=[B["sC"]])
                rr([c3_body(tl) for tl in tiles])
            mark("fc")
            wv, wb = w_next(("in", l, OFF["fc"], 512))
            bf_done = {}

            def fc_body(tl):
                tb = TB[tl["slot"]]
                B = tb["b"]
                pt, pb = proj(tb["hT"], B["hT"], 8, wv, wb, 0, 512)
                yield
                ft = tb["sgm"].bitcast(F32)
                fb = B["sgm"]
                act.op(lambda e, ft=ft, pt=pt: e.activation(out=ft[:], in_=pt[:], func=AF.Sigmoid), reads=[pb], writes=[fb])
                if l > 0:
                    dve.op(lambda e, ft=ft: e.tensor_tensor(out=ft[:], in0=ft[:], in1=oml[:], op=ALU.mult), reads=[fb, b_const], writes=[fb])
                    dve.op(lambda e, ft=ft: e.scalar_tensor_tensor(out=ft[:], in0=ft[:], scalar=TINY, in1=lbt[:], op0=ALU.max, op1=ALU.add),
                           reads=[fb, b_const], writes=[fb])
                else:
                    dve.op(lambda e, ft=ft: e.tensor_scalar_max(out=ft[:], in0=ft[:], scalar1=TINY), reads=[fb], writes=[fb])
                lf = tb["mg"][:, 512:1024]
                lfb = B["mg"]
                act.op(lambda e, lf=lf, ft=ft: e.activation(out=lf, in_=ft[:], func=AF.Ln), reads=[fb], writes=[lfb], partial=True)
                kk = ft
                kkb = fb
                dve.op(lambda e, kk=kk, ft=ft: e.tensor_scalar(out=kk[:], in0=ft[:], scalar1=-1.0, scalar2=1.0, op0=ALU.mult, op1=ALU.add),
                       reads=[fb], writes=[kkb])
                yield
                egl = egl_all[:, tl["slot"], :, :]
                b_egl = tb["b"].setdefault("egl", Buf("egl"))
                gps = []
                for m in range(3):
                    gt_, gb_ = PS.get()
                    pe.op(lambda e, gt_=gt_, m=m, lf=lf: e.matmul(gt_[:], lhsT=hm[kind][:, m, :], rhs=lf, start=True, stop=True),
                          reads=[lfb, b_const], writes=[gb_])
                    gps.append((gt_, gb_))
                glt, glb = PS.get()
                for h in range(4):
                    pe.op(lambda e, h=h, glt=glt, lf=lf: e.matmul(glt[:, h * 16:h * 16 + NC], lhsT=lf[:, h * 128:(h + 1) * 128],
                                                                  rhs=csel[kind][:], start=True, stop=True),
                          reads=[lfb, b_const], writes=[glb], partial=(h > 0), inc=(h == 3))
                act.op(lambda e, glt=glt: e.activation(out=egl[:, :, 0:NC], in_=glt[:, 0:64].rearrange("p (h c) -> p h c", h=4)[:, :, 0:NC],
                                                       func=AF.Exp), reads=[glb], writes=[b_egl])
                exps = []
                for (m, sc_) in ((0, 1.0), (1, 1.0), (1, -1.0), (2, 1.0)):
                    et, eb = SF.get()
                    act.op(lambda e, et=et, m=m, sc_=sc_: e.activation(out=et[:], in_=gps[m][0][:], func=AF.Exp, scale=sc_),
                           reads=[gps[m][1]], writes=[eb])
                    exps.append((et, eb))
                Qt = tb["qT"][:].rearrange("p a b -> p (a b)")
                Qtb = B["qT"]
                dve.op(lambda e, Qt=Qt, tb=tb: e.tensor_tensor(out=Qt, in0=tb["sA"][:], in1=exps[1][0][:], op=ALU.mult),
                       reads=[B["sA"], exps[1][1]], writes=[Qtb])
                Qg = tb["sA"]
                Qgb = B["sA"]
                dve.op(lambda e, Qg=Qg, tb=tb: e.tensor_tensor(out=Qg[:], in0=tb["sA"][:], in1=exps[0][0][:], op=ALU.mult),
                       reads=[B["sA"], exps[0][1]], writes=[Qgb])
                Kt = tb["yT"][:, 8:12, :].rearrange("p a b -> p (a b)")
                Ktb = B["yT"]
                dve.op(lambda e, Kt=Kt, kk=kk: e.tensor_tensor(out=Kt, in0=kk[:], in1=exps[2][0][:], op=ALU.mult),
                       reads=[kkb, exps[2][1]], writes=[Ktb], partial=True)
                Kd = tb["mg"][:, 512:768].bitcast(BF16)
                Kdb = B["mg"]
                dve.op(lambda e, Kd=Kd, kk=kk: e.tensor_tensor(out=Kd, in0=kk[:], in1=exps[3][0][:], op=ALU.mult),
                       reads=[kkb, exps[3][1]], writes=[Kdb], partial=True)
                iv = tb["sB"]
                ivb = B["sB"]
                yield
                def bd(dst):
                    return AP(dst[:].tensor, dst[:].offset, [list(dst[:].ap[0]), [NC * 128, 4], [128 + L, NC], [1, L]])

                def bfront(tl2):
                    tb2 = TB[tl2["slot"]]
                    B2 = tb2["b"]
                    srcs = ((tb2["sA"], B2["sA"]), (tb2["qT"][:].rearrange("p a b -> p (a b)"), B2["qT"]),
                            (tb2["yT"][:, 8:12, :].rearrange("p a b -> p (a b)"), B2["yT"]))
                    tps = []
                    for (src, srcb) in srcs:
                        tp_, tpb = PS.get()
                        for j in range(4):
                            pe.op(lambda e, j=j, tp_=tp_, src=src: e.matmul(tp_[:, j * 128:(j + 1) * 128], lhsT=src[:, j * 128:(j + 1) * 128],
                                                                            rhs=ident[:], start=True, stop=True),
                                  reads=[srcb, b_const], writes=[tpb], partial=(j > 0), inc=(j == 3))
                        tps.append((tp_, tpb))
                    dve.op(lambda e: e.tensor_copy(out=QtT[:], in_=tps[1][0][:].rearrange("p (h t) -> p h t", h=4)),
                           reads=[tps[1][1]], writes=[b_QtT])
                    qg_tmp, qg_tmpb = SH.get()
                    act.op(lambda e: e.copy(out=qg_tmp[:], in_=tps[0][0][:]), reads=[tps[0][1]], writes=[qg_tmpb])
                    act.op(lambda e: e.copy(out=bd(KtT_m), in_=tps[2][0][:].rearrange("p (h c j) -> p h c j", h=4, c=NC)),
                           reads=[tps[2][1]], writes=[b_KtT], partial=True)
                    at_, atb = PS.get()
                    a3_ = at_[:].rearrange("p (h t) -> p h t", h=4)
                    for h in range(4):
                        for c in range(NC):
                            pe.op(lambda e, h=h, c=c: e.matmul(a3_[:, h, c * L:(c + 1) * L], lhsT=KtT_m[:, h, c, :],
                                                               rhs=QtT[:, h, c * L:(c + 1) * L], start=True, stop=True),
                                  reads=[b_KtT, b_QtT], writes=[atb], partial=not (h == 0 and c == 0),
                                  inc=(h == 3 and c == NC - 1))
                    dve.op(lambda e: e.scalar_tensor_tensor(out=aT_sb[:], in0=a3_, scalar=1e30, in1=bcast(cmask[kind][:], [[0, 4], [1, 128]]),
                                                            op0=ALU.min, op1=ALU.mult), reads=[atb, b_const], writes=[b_aT])
                    bf_done[tl2["slot"]] = (qg_tmp, qg_tmpb)

                if kind == "p":
                    if tl["slot"] not in bf_done:
                        bfront(tl)
                    qg_tmp, qg_tmpb = bf_done.pop(tl["slot"])
                    act.op(lambda e: e.copy(out=bd(QgT_m), in_=qg_tmp[:].rearrange("p (h c j) -> p h c j", h=4, c=NC)),
                           reads=[qg_tmpb], writes=[b_QgT], partial=True)
                    for c in range(NC):
                        dve.op(lambda e, Kd=Kd, c=c: e.tensor_scalar_mul(out=Kd_m[:, c, :], in0=Kd, scalar1=cselh[kind][:, c:c + 1]),
                               reads=[Kdb, b_const], writes=[b_Kd], partial=(c > 0))
                else:
                    tps = []
                    for (src, srcb) in ((Qg, Qgb), (Qt, Qtb), (Kt, Ktb)):
                        tp_, tpb = PS.get()
                        for j in range(4):
                            pe.op(lambda e, j=j, tp_=tp_, src=src: e.matmul(tp_[:, j * 128:(j + 1) * 128], lhsT=src[:, j * 128:(j + 1) * 128],
                                                                            rhs=ident[:], start=True, stop=True),
                                  reads=[srcb, b_const], writes=[tpb], partial=(j > 0), inc=(j == 3))
                        tps.append((tp_, tpb))
                    dve.op(lambda e: e.tensor_copy(out=QtT[:], in_=tps[1][0][:].rearrange("p (h t) -> p h t", h=4)),
                           reads=[tps[1][1]], writes=[b_QtT])
                    QgT_s = sb_s["QgT_s"]
                    KtT_s = sb_s["KtT_s"]
                    act.op(lambda e: e.copy(out=QgT_s[:], in_=tps[0][0][:].rearrange("p (h t) -> p h t", h=4)),
                           reads=[tps[0][1]], writes=[sb_s["b_QgT_s"]])
                    act.op(lambda e: e.copy(out=KtT_s[:], in_=tps[2][0][:].rearrange("p (h t) -> p h t", h=4)),
                           reads=[tps[2][1]], writes=[sb_s["b_KtT_s"]])
                    at_, atb = PS.get()
                    a3 = at_[:].rearrange("p (h t) -> p h t", h=4)
                    for h in range(4):
                        pe.op(lambda e, h=h: e.matmul(a3[:, h, :], lhsT=sb_s["KtT_s"][:, h, :], rhs=QtT[:, h, :], start=True, stop=True),
                              reads=[sb_s["b_KtT_s"], b_QtT], writes=[atb], partial=(h > 0), inc=(h == 3))
                    dve.op(lambda e: e.scalar_tensor_tensor(out=aT_sb[:], in0=a3, scalar=1e30, in1=bcast(cmask[kind][:], [[0, 4], [1, 128]]),
                                                            op0=ALU.min, op1=ALU.mult), reads=[atb, b_const], writes=[b_aT])
                ot, ob = PSO.get()
                o3 = ot[:].rearrange("p (h v) -> p h v", h=4)
                if kind == "p":
                    for h in range(4):
                        pe.op(lambda e, h=h: e.matmul(o3[:, h, :], lhsT=aT_sb[:, h, :], rhs=iv[:, h * 128:(h + 1) * 128],
                                                      start=(h == 0), stop=False, skip_group_check=True),
                              reads=[b_aT, ivb], writes=[ob], partial=(h > 0), inc=(h == 3))
                    for c0 in range(0, NC, 4):
                        sus = []
                        for c in range(c0, c0 + 4):
                            su, sub = PS.get()
                            s3 = su[:].rearrange("p (h v) -> p h v", h=4)
                            for h in range(4):
                                pe.op(lambda e, h=h, c=c, s3=s3: e.matmul(s3[:, h, :], lhsT=Kd_m[:, c, h * 128:(h + 1) * 128],
                                                                          rhs=iv[:, h * 128:(h + 1) * 128], start=True, stop=True),
                                      reads=[b_Kd, ivb], writes=[sub], partial=(h > 0), inc=(h == 3))
                            sus.append((s3, sub))
                        for c in range(c0, c0 + 4):
                            s3, sub = sus[c - c0]
                            q_ = c % 2
                            for h in range(4):
                                pe.op(lambda e, h=h, c=c, q_=q_: e.matmul(o3[:, h, :], lhsT=QgT_m[:, h, c, :], rhs=Sbf_p[l][q_][:, h, :],
                                                                          start=False, stop=(c == NC - 1), skip_group_check=True),
                                      reads=[b_QgT, b_Sbfp[l][q_][h]], writes=[ob], partial=True)
                            for h in range(4):
                                dve.op(lambda e, h=h, c=c, s3=s3: e.scalar_tensor_tensor(
                                    out=S_p[l][:, h, :], in0=S_p[l][:, h, :], scalar=egl[:, h, c:c + 1], in1=s3[:, h, :],
                                    op0=ALU.mult, op1=ALU.add), reads=[b_Sp[l][h], b_egl, sub], writes=[b_Sp[l][h]])
                                act.op(lambda e, h=h, q_=q_: e.copy(out=Sbf_p[l][1 - q_][:, h, :], in_=S_p[l][:, h, :]),
                                       reads=[b_Sp[l][h]], writes=[b_Sbfp[l][1 - q_][h]])
                        if c0 == 0:
                            nxt = [t_ for t_ in tiles if t_["slot"] == tl["slot"] + 1]
                            if nxt:
                                bfront(nxt[0])
                    if tl["ti"] == nt - 1:
                        sp.dma(oclk.get(), hgp[l].rearrange("h k v -> k h v"), S_p[l][:], reads=b_Sp[l])
                else:
                    for h in range(4):
                        sp.dma(sclk[2], S_s[:], sth[l][:, h, :, :].rearrange("b k v -> k b v"), writes=[b_Ss])
                        pool.dma(sclk[3], Sbf_s[:], sth[l][:, h, :, :].rearrange("b k v -> k b v"), writes=[b_Sbfs])
                        bd16 = AP(QgT_ms[:].tensor, QgT_ms[:].offset, [list(QgT_ms[:].ap[0]), [128 + 8, 16], [1, 8]])
                        act.op(lambda e, h=h, bd16=bd16: e.copy(out=bd16, in_=sb_s["QgT_s"][:, h, :].rearrange("p (c j) -> p c j", c=16)),
                               reads=[sb_s["b_QgT_s"]], writes=[b_QgTs], partial=True)
                        dve.op(lambda e, h=h, Kd=Kd: e.tensor_tensor(out=Kd_ms[:], in0=bcast(Kd[:, h * 128:(h + 1) * 128], [[0, 16], [1, 128]]),
                                                                     in1=bcast(cselh["s"][:], [[1, 16], [0, 128]]), op=ALU.mult),
                               reads=[Kdb, b_const], writes=[b_Kds])
                        for b in range(16):
                            pe.op(lambda e, h=h, b=b: e.matmul(o3[:, h, :], lhsT=QgT_ms[:, b, :], rhs=Sbf_s[:, b, :],
                                                               start=(b == 0 and h == 0), stop=False, skip_group_check=True),
                                  reads=[b_QgTs, b_Sbfs], writes=[ob], partial=not (h == 0 and b == 0), inc=(b == 15))
                        pe.op(lambda e, h=h: e.matmul(o3[:, h, :], lhsT=aT_sb[:, h, :], rhs=iv[:, h * 128:(h + 1) * 128],
                                                      start=False, stop=True, skip_group_check=True), reads=[b_aT, ivb], writes=[ob], partial=True)
                        for b4 in range(0, 16, 4):
                            su, sub = PS.get()
                            s3 = su[:].rearrange("p (b v) -> p b v", b=4)
                            for bb in range(4):
                                pe.op(lambda e, h=h, bb=bb, b4=b4, s3=s3: e.matmul(s3[:, bb, :], lhsT=Kd_ms[:, b4 + bb, :],
                                                                                   rhs=iv[:, h * 128:(h + 1) * 128], start=True, stop=True),
                                      reads=[b_Kds, ivb], writes=[sub], partial=(bb > 0), inc=(bb == 3))
                            for bb in range(4):
                                dve.op(lambda e, h=h, bb=bb, b4=b4, s3=s3: e.scalar_tensor_tensor(
                                    out=S_s[:, b4 + bb, :], in0=S_s[:, b4 + bb, :], scalar=egl[:, h, b4 + bb:b4 + bb + 1],
                                    in1=s3[:, bb, :], op0=ALU.mult, op1=ALU.add), reads=[b_Ss, b_egl, sub], writes=[b_Ss], partial=True)
                        sp.dma(oclk.get(), hgs[l][:, h, :, :].rearrange("b k v -> k b v"), S_s[:], reads=[b_Ss])
                oc = tb["mg"][:, 0:512]
                act.op(lambda e, oc=oc, ot=ot: e.copy(out=oc, in_=ot[:]), reads=[ob], writes=[B["mg"]], partial=True)
                yield
                ob = B["mg"]
                o3 = oc.rearrange("p (h v) -> p h v", h=4)
                sq, sqb = SF.get()
                act.op(lambda e, sq=sq, oc=oc: e.activation(out=sq[:], in_=oc, func=AF.Square), reads=[ob], writes=[sqb])
                ss, ssb = SM.get()
                dve.op(lambda e, ss=ss, sq=sq: e.tensor_reduce(out=ss[:, 0:4], in_=sq[:].rearrange("p (h d) -> p h d", h=4),
                                                               axis=AX.X, op=ALU.add), reads=[sqb], writes=[ssb])
                rstd_from_ss(ss, ssb, 4, 512, DK)
                yc, ycb = SF.get()
                dve.op(lambda e, yc=yc, o3=o3, ss=ss: e.tensor_tensor(out=yc[:].rearrange("p (h d) -> p h d", h=4), in0=o3,
                                                                      in1=bcast(ss[:, 0:4], [[1, 4], [0, 128]]), op=ALU.mult),
                       reads=[ob, ssb], writes=[ycb])
                dbg("yc", yc[:], ycb, tl, l, kind)
                dbg("lf", lf, lfb, tl, l, kind)
                yg, ygb = SH.get()
                dve.op(lambda e, yg=yg, yc=yc, tb=tb: e.tensor_tensor(out=yg[:], in0=yc[:], in1=tb["sC"][:], op=ALU.mult),
                       reads=[ycb, B["sC"]], writes=[ygb])
                transposes(yg, ygb, 4, tb["yT"], B["yT"], 8)
            rr([fc_body(tl) for tl in tiles])
            mark("merge")
            for br in range(3):
                for half in range(2):
                    wv, wb = w_next(("in", l, OFF[("ma", "mb", "mc")[br]] + 512 * half, 512))

                    def mg_body(tl, half=half, wv=wv, wb=wb):
                        tb = TB[tl["slot"]]
                        B = tb["b"]
                        pt, pb = proj(tb["hT"], B["hT"], 8, wv, wb, 0, 512)
                        yield
                        act.op(lambda e, tb=tb, pt=pt, half=half: e.activation(out=tb["sgm"][:, half * 512:(half + 1) * 512], in_=pt[:],
                                                                               func=AF.Sigmoid), reads=[pb], writes=[B["sgm"]], partial=(half == 1))
                    rr([mg_body(tl) for tl in tiles])
                wv, wb = w_next(("br", l, br, 0))

                def br_body(tl, br=br, wv=wv, wb=wb):
                    tb = TB[tl["slot"]]
                    B = tb["b"]
                    pps = []
                    for half in range(2):
                        pt, pb = PS.get()
                        for c in range(4):
                            pe.op(lambda e, c=c, pt=pt, tb=tb, half=half, br=br: e.matmul(
                                pt[:], lhsT=tb["yT"][:, 4 * br + c, :], rhs=wv[:, c, half * 512:(half + 1) * 512],
                                start=(c == 0), stop=(c == 3)), reads=[B["yT"], wb], writes=[pb], partial=(c > 0), inc=(c == 3))
                        pps.append((pt, pb))
                    yield
                    for half in range(2):
                        pt, pb = pps[half]
                        mg_h = tb["mg"][:, half * 512:(half + 1) * 512]
                        sg_h = tb["sgm"][:, half * 512:(half + 1) * 512]
                        if br == 0:
                            dve.op(lambda e, mg_h=mg_h, pt=pt, sg_h=sg_h: e.tensor_tensor(out=mg_h, in0=pt[:], in1=sg_h, op=ALU.mult),
                                   reads=[pb, B["sgm"]], writes=[B["mg"]], partial=True)
                        else:
                            tt, ttb = SF.get()
                            dve.op(lambda e, tt=tt, pt=pt, sg_h=sg_h: e.tensor_tensor(out=tt[:], in0=pt[:], in1=sg_h, op=ALU.mult),
                                   reads=[pb, B["sgm"]], writes=[ttb])
                            pool.op(lambda e, mg_h=mg_h, tt=tt: e.tensor_tensor(out=mg_h, in0=mg_h, in1=tt[:], op=ALU.add),
                                    reads=[ttb, B["mg"]], writes=[B["mg"]], partial=True)
                    if br == 2:
                        dbg("mg", tb["mg"][:], B["mg"], tl, l, kind)
                        mh, mhb = BIGH.get()
                        act.op(lambda e, mh=mh, tb=tb: e.copy(out=mh[:], in_=tb["mg"][:]), reads=[B["mg"]], writes=[mhb])
                        yield
                        transposes(mh, mhb, 8, tb["hT"], B["hT"], 0, full_overwrite=True)
                rr([br_body(tl) for tl in tiles])
            mark("out")
            for half in range(2):
                wv, wb = w_next(("out", l, 512 * half, 512))

                def out_body(tl, half=half, wv=wv, wb=wb):
                    tb = TB[tl["slot"]]
                    B = tb["b"]
                    pt, pb = proj(tb["hT"], B["hT"], 8, wv, wb, 0, 512)
                    yield
                    tt, ttb = SF.get()
                    dve.op(lambda e, tt=tt, pt=pt, half=half: e.tensor_tensor(out=tt[:], in0=pt[:], in1=lc["gate"][:, half * 512:(half + 1) * 512],
                                                                             op=ALU.mult), reads=[pb, lc["b_mod"]], writes=[ttb])
                    xh = tb["x"][:, half * 512:(half + 1) * 512]
                    dve.op(lambda e, xh=xh, tt=tt: e.tensor_tensor(out=xh, in0=xh, in1=tt[:], op=ALU.add),
                           reads=[ttb, B["x"]], writes=[B["x"]], partial=True)
                    if half == 1:
                        dbg("xo", tb["x"][:], B["x"], tl, l, kind)
                    if last_layer and half == 1:
                        dst = ys if kind == "s" else yp[tl["ti"] * 128:(tl["ti"] + 1) * 128, :]
                        sp.dma(yclk[tl["slot"]], dst, tb["x"][:], reads=[B["x"]])
                rr([out_body(tl) for tl in tiles])

    groups = []
    t = 0
    while t < nt:
        g = list(range(t, min(nt, t + G)))
        groups.append(g)
        t += G
    sb_s = {}
    compute_mods(cp)
    for g in groups:
        run_group([dict(ti=ti, slot=i) for i, ti in enumerate(g)], "p")
    if do_sample:
        fw.barrier()
        cv_ = Carver(ARENA_TILE)
        kTc = cv_.take([128, 16, 128], BF16)
        Vc = cv_.take([128, 16, 2, 65], BF16)
        S_s = cv_.take([128, 16, 128])
        Sbf_s = cv_.take([128, 16, 128], BF16)
        QgT_ms = cv_.take([128, 16, 128], BF16)
        Kd_ms = cv_.take([128, 16, 128], BF16)
        sb_s["QgT_s"] = cv_.take([128, 4, 128], BF16)
        sb_s["KtT_s"] = cv_.take([128, 4, 128], BF16)
        sb_s["b_QgT_s"] = Buf("QgT_s")
        sb_s["b_KtT_s"] = Buf("KtT_s")
        b_kTc, b_Vc, b_Ss, b_Sbfs, b_QgTs, b_Kds = [Buf(n) for n in ("kTc", "Vc", "Ss", "Sbfs", "QgTs", "Kds")]
        dve.op(lambda e: e.memset(Vc[:], 1.0), writes=[b_Vc])
        dve.op(lambda e: e.memset(QgT_ms[:], 0.0), writes=[b_QgTs])
        for l in range(DEPTH):
            cload(pool, cclk2, LC[l]["wsT"][:], wsT_s[l].rearrange("g s t -> s g t"))
            dve.op(lambda e, l=l: e.tensor_tensor(out=LC[l]["wsT"][:], in0=LC[l]["wsT"][:],
                                                  in1=bcast(smask["s"][:], [[0, 8], [1, 128]]), op=ALU.mult),
                   reads=[b_const], writes=[b_const], partial=True)
        compute_mods(cs_)
        run_group([dict(ti=None, slot=0)], "s")
    fw.finish()
    return nc, fw, worder_out


def _prep_inputs(inputs, nt):
    f = lambda a: np.ascontiguousarray(np.asarray(a, dtype=np.float32))
    I = {k: np.asarray(v) for k, v in inputs.items()}
    consts = _consts(nt)
    shared = {}
    for k in ("norm_g", "w_ada", "b_ada", "w_in", "q_norm_g", "k_norm_g", "sinks", "sg_norm_g", "hgrn_lb",
              "hgrn_norm_g", "w_br_a", "w_br_b", "w_br_c", "w_out"):
        shared[k] = f(I[k])
    ws = I["w_spatial"]
    shared["wsT_p"] = f(ws.transpose(0, 1, 3, 2))
    shared["wsT_s"] = f(np.tile(ws[:, :, :8, :8].transpose(0, 1, 3, 2), (1, 1, 16, 16)))
    bs = I["b_spatial"]
    shared["bsp_p"] = f(bs.transpose(0, 2, 1))
    shared["bsp_s"] = f(np.tile(bs[:, :, :8].transpose(0, 2, 1), (1, 16, 1)))
    for k, v in consts.items():
        shared["c_" + k] = f(v)
    maps = []
    for c in range(8):
        m = dict(shared)
        m["xp"] = f(I["x_prompt"][c, :nt * 128])
        m["xs"] = f(I["x_sample"][16 * c:16 * c + 16].reshape(128, D))
        m["cp"] = f(np.broadcast_to(I["c_prompt"][c][None, :], (128, D)))
        m["cs"] = f(np.repeat(I["c_sample"][16 * c:16 * c + 16], 8, axis=0))
        ckc = I["cache_win_k"][:, 16 * c:16 * c + 16]
        m["ckT"] = f(ckc.transpose(0, 1, 3, 4, 2))
        m["ck"] = f(ckc.reshape(DEPTH, 16, 128, 128))
        m["cv"] = f(I["cache_win_v"][:, 16 * c:16 * c + 16].reshape(DEPTH, 16, 128, 128))
        m["sth"] = f(I["state_hgrn"][:, 16 * c:16 * c + 16])
        maps.append(m)
    return maps


def _assemble(results, nt):
    yp = np.stack([r["yp"] for r in results]).reshape(8, nt * 128, D)
    ys = np.concatenate([r["ys"].reshape(16, 8, D) for r in results], 0)
    wkp = np.stack([r["wkp"] for r in results], 1).reshape(DEPTH, 8, 128, 2, 64)
    wvp = np.stack([r["wvp"] for r in results], 1).reshape(DEPTH, 8, 128, 2, 64)
    hgp = np.stack([r["hgp"] for r in results], 1)
    wks = np.concatenate([r["wks"] for r in results], 1).reshape(DEPTH, 128, 128, 2, 64)
    wvs = np.concatenate([r["wvs"] for r in results], 1).reshape(DEPTH, 128, 128, 2, 64)
    hgs = np.concatenate([r["hgs"] for r in results], 1)
    sgv = np.concatenate([r["sgv"].reshape(DEPTH, 16, 8, 512) for r in results], 1)
    return (yp, ys, wkp, wvp, hgp, wks, wvs, hgs, sgv)


def kernel(**inputs):
    nt = NTILES
    _, _, order = build_program(nt=nt, G=3, do_sample=True)
    nc, _, _ = build_program(nt=nt, G=3, do_sample=True, worder=order)
    maps = _prep_inputs(inputs, nt)
    res = run_bass_kernel_spmd(nc, maps, core_ids=list(range(8)))
    outs = _assemble(res.results, nt)
    return tuple(np.ascontiguousarray(o.astype(np.float32)) for o in outs)
```

```python
import numpy as np
import concourse.bass as bass
import concourse.mybir as mybir
from concourse.ap import AP
from concourse.bass_utils import run_bass_kernel_spmd

F32 = mybir.dt.float32
BF16 = mybir.dt.bfloat16
ALU = mybir.AluOpType
AF = mybir.ActivationFunctionType
AX = mybir.AxisListType

D = 1024
DEPTH = 2
SEQ = 4096
NTILES = SEQ // 128
PAST = 16384
HD = 64
NQH = 8
NKV = 2
WA = 512
DK = 128
HH = 4
EPS = 1e-6
TINY = float(np.finfo(np.float32).tiny)
THETA = 500000.0
IN_W = 7936
OFF = dict(qa=0, ka=512, va=640, ga=768, ub=1280, vb=1792, gb=2304, qc=2816, fc=3328, ic=3840,
           gc=4352, ma=4864, mb=5888, mc=6912)


class Clock:
    __slots__ = ("sem", "val", "name")

    def __init__(self, sem, name):
        self.sem = sem
        self.val = 0
        self.name = name


class Buf:
    __slots__ = ("w", "r", "name", "dead")

    def __init__(self, name=""):
        self.w = {}
        self.r = {}
        self.name = name
        self.dead = False


class _Rec:
    def __init__(self):
        self.call = None

    def __getattr__(self, name):
        def f(*a, **k):
            self.call = (name, a, k)
            return self
        return f


class Eng:
    def __init__(self, name, sem):
        self.name = name
        self.clock = Clock(sem, name)
        self.seen = {}
        self.prog = []
        self.nops = 0

    def _wait(self, clock, val):
        if val <= 0:
            return
        if clock is self.clock and (self.name == "pe" or val > self.clock.val):
            return
        if self.seen.get(clock, 0) >= val:
            return
        self.seen[clock] = val
        self.prog.append(("wait", clock, val))

    def deps(self, reads=(), writes=()):
        for b in reads:
            assert not b.dead, "stale ring handle read: " + b.name
            for c, v in b.w.items():
                self._wait(c, v)
        for b in writes:
            assert not b.dead, "stale ring handle write: " + b.name
            for c, v in b.w.items():
                self._wait(c, v)
            for c, v in b.r.items():
                self._wait(c, v)

    @staticmethod
    def _mark(clock, val, reads, writes, partial):
        for b in reads:
            if b.r.get(clock, 0) < val:
                b.r[clock] = val
        for b in writes:
            if not partial:
                b.w = {}
            b.r = {}
            b.w[clock] = val

    def op(self, fn, reads=(), writes=(), partial=False, inc=True):
        rec = _Rec()
        fn(rec)
        name, a, k = rec.call
        self.deps(reads, writes)
        self.nops += 1
        if inc:
            self.clock.val += 1
            self.prog.append(("op", name, a, k, self.clock.val))
            self._mark(self.clock, self.clock.val, reads, writes, partial)
        else:
            self.prog.append(("op", name, a, k, None))
            self._mark(self.clock, self.clock.val + 1, reads, writes, partial)

    def dma(self, dclock, out, in_, reads=(), writes=(), partial=False):
        self.deps(reads, writes)
        self._wait(dclock, dclock.val)
        dclock.val += 16
        self.prog.append(("dma", out, in_, dclock.sem))
        self._mark(dclock, dclock.val, reads, writes, partial)


class FW:
    def __init__(self, nc):
        self.nc = nc
        self.pe = Eng("pe", nc.alloc_semaphore("s_pe"))
        self.act = Eng("act", nc.alloc_semaphore("s_act"))
        self.dve = Eng("dve", nc.alloc_semaphore("s_dve"))
        self.pool = Eng("pool", nc.alloc_semaphore("s_pool"))
        self.sp = Eng("sp", nc.alloc_semaphore("s_sp"))
        self.engs = (self.pe, self.act, self.dve, self.pool, self.sp)
        self.dclocks = []
        self._n = 0

    def dclock(self, name):
        c = Clock(self.nc.alloc_semaphore("d_" + name), name)
        self.dclocks.append(c)
        return c

    def barrier(self):
        for e in self.engs:
            for o in self.engs:
                if o is not e:
                    e._wait(o.clock, o.clock.val)
            for c in self.dclocks:
                e._wait(c, c.val)

    def finish(self):
        for c in self.dclocks:
            self.sp._wait(c, c.val)
        for e in (self.pe, self.act, self.dve, self.pool):
            self.sp._wait(e.clock, e.clock.val)
        eclocks = {e.clock: e for e in self.engs}
        waited = {e.clock: set() for e in self.engs}
        for e in self.engs:
            for ent in e.prog:
                if ent[0] == "wait" and ent[1] in waited:
                    waited[ent[1]].add(ent[2])
        semval = {}
        for c, ws in waited.items():
            semval[c] = {v: i + 1 for i, v in enumerate(sorted(ws))}
        self.n_inc = {eclocks[c].name: len(ws) for c, ws in waited.items()}

        def emit(eng_obj, e):
            mine = semval[eng_obj.clock]
            for ent in eng_obj.prog:
                if ent[0] == "wait":
                    c, v = ent[1], ent[2]
                    e.wait_ge(c.sem, semval[c][v] if c in semval else v)
                elif ent[0] == "op":
                    ins = getattr(e, ent[1])(*ent[2], **ent[3])
                    if ent[4] is not None and ent[4] in mine:
                        ins.then_inc(eng_obj.clock.sem, 1)
                else:
                    e.dma_start(out=ent[1], in_=ent[2]).then_inc(ent[3], 16)
        with self.nc.Block() as block:
            @block.tensor
            def _(e):
                emit(self.pe, e)

            @block.scalar
            def _(e):
                emit(self.act, e)

            @block.vector
            def _(e):
                emit(self.dve, e)

            @block.gpsimd
            def _(e):
                emit(self.pool, e)

            @block.sync
            def _(e):
                emit(self.sp, e)


class Ring:
    def __init__(self, nc, name, n, shape, dtype, psum=False):
        self.items = []
        for i in range(n):
            if psum:
                t = nc.alloc_psum_tensor(f"{name}{i}", shape, dtype)
            else:
                t = nc.alloc_sbuf_tensor(f"{name}{i}", shape, dtype)
            self.items.append((t, Buf(f"{name}{i}")))
        self.i = 0

    def get(self):
        t, b = self.items[self.i]
        nb = Buf(b.name)
        nb.w = dict(b.w)
        nb.r = dict(b.r)
        b.dead = True
        self.items[self.i] = (t, nb)
        self.i = (self.i + 1) % len(self.items)
        return t, nb


def bcast(a, dims):
    return AP(a.tensor, a.offset, [list(a.ap[0])] + [list(d) for d in dims])


def _consts(nt):
    c = {}
    j = np.arange(128)
    c["ident"] = np.eye(128, dtype=np.float32)
    mp = (j[:, None] > j[None, :]).astype(np.float32)
    mc = (j[:, None] <= j[None, :]).astype(np.float32)
    c["amask_p"] = np.stack([mp, mc], 1).copy()
    bq = j // 8
    iq = j % 8
    ms = np.zeros((128, 17, 128), np.float32)
    for b in range(16):
        ms[:, b, :] = (bq[None, :] == b) & (j[:, None] > iq[None, :])
    ms[:, 16, :] = (bq[:, None] == bq[None, :]) & (iq[:, None] <= iq[None, :])
    c["amask_s"] = ms
    inv = (np.float32(THETA) ** (-np.arange(8, dtype=np.float32) / np.float32(8))).astype(np.float32)

    def rope(pos):
        ang = pos.astype(np.float32)[:, None] * inv[None, :]
        co = np.cos(ang).astype(np.float32)
        si = np.sin(ang).astype(np.float32)
        return np.concatenate([co, co], 1), np.concatenate([-si, si], 1)
    cs, sn = rope(np.arange(nt * 128))
    c["ropec_p"] = cs.reshape(nt, 128, 16).transpose(1, 0, 2).copy()
    c["ropes_p"] = sn.reshape(nt, 128, 16).transpose(1, 0, 2).copy()
    cs, sn = rope(PAST + iq)
    c["ropec_s"] = cs.reshape(128, 1, 16).copy()
    c["ropes_s"] = sn.reshape(128, 1, 16).copy()
    for kind, L in (("p", 16), ("s", 8)):
        ch = j // L
        loc = j % L
        ref = L // 2 - 1
        same = ch[:, None] == ch[None, :]
        le = loc[:, None] <= loc[None, :]
        MG = (same & le).astype(np.float32)
        MR = (same & (loc[:, None] <= ref)).astype(np.float32)
        ML = (same & (loc[:, None] > loc[None, :])).astype(np.float32)
        c["hm_" + kind] = np.stack([MG, MG - MR, ML], 1).copy()
        nc_ = 128 // L
        sel = (ch[:, None] == np.arange(nc_)[None, :]).astype(np.float32)
        c["csel_" + kind] = sel.copy()
        c["cmask_" + kind] = (same & le).astype(np.float32)
    c["smask_p"] = (j[:, None] <= j[None, :]).astype(np.float32)
    c["smask_s"] = ((bq[:, None] == bq[None, :]) & (iq[:, None] <= iq[None, :])).astype(np.float32)
    return c


def build_program(nt=NTILES, G=3, do_sample=True, debug=None, worder=None):
    nc = bass.Bass("TRN2", target_bir_lowering=False)
    fw = FW(nc)
    dbg_clk = []

    def dbg(name, ap_, buf, tl, l, kind):
        if debug is None or (kind, l) != debug or (kind == "p" and tl["ti"] != 0):
            return
        n = ap_.shape[-1] if len(ap_.shape) == 2 else int(np.prod(ap_.shape[1:]))
        dt_ = nc.dram_tensor("dbg_" + name, [ap_.shape[0], n], ap_.dtype, kind="ExternalOutput").ap()
        if not dbg_clk:
            dbg_clk.append(fw.dclock("dbg"))
        src = ap_ if len(ap_.shape) == 2 else ap_.rearrange("p a b -> p (a b)")
        fw.sp.dma(dbg_clk[0], dt_, src, reads=[buf])
    pe, act, dve, pool, sp = fw.pe, fw.act, fw.dve, fw.pool, fw.sp

    def din(name, shape):
        return nc.dram_tensor(name, list(shape), F32, kind="ExternalInput").ap()

    def dout(name, shape):
        return nc.dram_tensor(name, list(shape), F32, kind="ExternalOutput").ap()

    xp = din("xp", [nt * 128, D])
    xs = din("xs", [128, D])
    cp = din("cp", [128, D])
    cs_ = din("cs", [128, D])
    ckT = din("ckT", [DEPTH, 16, 2, 64, 128])
    ck = din("ck", [DEPTH, 16, 128, 128])
    cv = din("cv", [DEPTH, 16, 128, 128])
    sth = din("sth", [DEPTH, 16, 4, 128, 128])
    norm_g = din("norm_g", [DEPTH, D])
    w_ada = din("w_ada", [DEPTH, D, 3 * D])
    b_ada = din("b_ada", [DEPTH, 3 * D])
    w_in = din("w_in", [DEPTH, D, IN_W])
    q_norm_g = din("q_norm_g", [DEPTH, HD])
    k_norm_g = din("k_norm_g", [DEPTH, HD])
    sinks = din("sinks", [DEPTH, NQH])
    sg_norm_g = din("sg_norm_g", [DEPTH, 512])
    wsT_p = din("wsT_p", [DEPTH, 8, 128, 128])
    wsT_s = din("wsT_s", [DEPTH, 8, 128, 128])
    bsp_p = din("bsp_p", [DEPTH, 128, 8])
    bsp_s = din("bsp_s", [DEPTH, 128, 8])
    hgrn_lb = din("hgrn_lb", [DEPTH, 512])
    hgrn_norm_g = din("hgrn_norm_g", [DEPTH, 128])
    w_br = [din("w_br_a", [DEPTH, 512, D]), din("w_br_b", [DEPTH, 512, D]), din("w_br_c", [DEPTH, 512, D])]
    w_out = din("w_out", [DEPTH, D, D])
    cshape = {k: v.shape for k, v in _consts(1).items()}
    cshape["ropec_p"] = (128, nt, 16)
    cshape["ropes_p"] = (128, nt, 16)
    cd = {k: din("c_" + k, list(s)) for k, s in cshape.items()}

    yp = dout("yp", [nt * 128, D])
    ys = dout("ys", [128, D])
    wkp = dout("wkp", [DEPTH, 128, 128])
    wvp = dout("wvp", [DEPTH, 128, 128])
    hgp = dout("hgp", [DEPTH, 4, 128, 128])
    wks = dout("wks", [DEPTH, 16, 128, 128])
    wvs = dout("wvs", [DEPTH, 16, 128, 128])
    hgs = dout("hgs", [DEPTH, 16, 4, 128, 128])
    sgv = dout("sgv", [DEPTH, 128, 512])

    def sb(name, shape, dt=F32):
        return nc.alloc_sbuf_tensor(name, list(shape), dt)

    PS = Ring(nc, "ps", 6, [128, 512], F32, psum=True)
    PSO = Ring(nc, "pso", 2, [128, 512], F32, psum=True)
    SF = Ring(nc, "sf", 8, [128, 512], F32)
    SH = Ring(nc, "sh", 8, [128, 512], BF16)
    BIGF = Ring(nc, "bigf", 2, [128, D], F32)
    BIGH = Ring(nc, "bigh", 3, [128, D], BF16)
    SM = Ring(nc, "sm", 12, [128, 16], F32)
    NSLOT = 3
    WS = Ring(nc, "wslot", NSLOT, [128, 4096], BF16)
    wclk = [fw.dclock(f"w{i}") for i in range(NSLOT)]
    wclk_hw = [fw.dclock(f"wh{i}") for i in range(NSLOT)]

    ident = sb("ident", [128, 128], BF16)
    amask_p = sb("amask_p", [128, 2, 128], BF16)
    amask_s = sb("amask_s", [128, 17, 128], BF16)
    ropec_p = sb("ropec_p", [128, G, 16])
    ropes_p = sb("ropes_p", [128, G, 16])
    b_rope = Buf("rope")
    ropec_s = sb("ropec_s", [128, 1, 16])
    ropes_s = sb("ropes_s", [128, 1, 16])
    hm = {k: sb("hm_" + k, [128, 3, 128]) for k in "ps"}
    csel = {"p": sb("csel_p", [128, 8]), "s": sb("csel_s", [128, 16])}
    cselh = {"p": sb("cselh_p", [128, 8], BF16), "s": sb("cselh_s", [128, 16], BF16)}
    cmask = {k: sb("cmask_" + k, [128, 128], BF16) for k in "ps"}
    smask = {k: sb("smask_" + k, [128, 128], BF16) for k in "ps"}
    b_const = Buf("const")
    epsb = sb("epsb", [128, 1])
    class RR:
        def __init__(self, name, n):
            self.c = [fw.dclock(f"{name}{i}") for i in range(n)]
            self.i = 0

        def get(self):
            self.i = (self.i + 1) % len(self.c)
            return self.c[self.i]
    cclk = RR("const", 6)
    cclk2 = RR("constp", 4)

    def cload(eng, clk, dst, src):
        eng.dma(clk.get(), dst, src, writes=[b_const], partial=True)

    cload(pool, cclk2, ident[:], cd["ident"])
    cload(pool, cclk2, amask_p[:], cd["amask_p"])
    cload(pool, cclk2, amask_s[:], cd["amask_s"])
    cload(sp, cclk, ropec_s[:], cd["ropec_s"])
    cload(sp, cclk, ropes_s[:], cd["ropes_s"])
    for k in "ps":
        cload(sp, cclk, hm[k][:], cd["hm_" + k])
        cload(sp, cclk, csel[k][:], cd["csel_" + k])
        cload(pool, cclk2, cselh[k][:], cd["csel_" + k])
        cload(pool, cclk2, cmask[k][:], cd["cmask_" + k])
        cload(pool, cclk2, smask[k][:], cd["smask_" + k])

    dve.op(lambda e: e.memset(epsb[:], EPS), writes=[b_const], partial=True)
    LC = []
    for l in range(DEPTH):
        lc = dict(
            qg=sb(f"qg{l}", [128, HD]), kg=sb(f"kg{l}", [128, HD]), sgg=sb(f"sgg{l}", [128, 512]),
            hgg=sb(f"hgg{l}", [128, 128]), esink=sb(f"esink{l}", [128, NQH]),
            bsp={"p": sb(f"bspp{l}", [128, 8]), "s": sb(f"bsps{l}", [128, 8])},
            wsT=sb(f"wsT{l}", [128, 8, 128], BF16),
            A=sb(f"modA{l}", [128, D]), shift=sb(f"modS{l}", [128, D]), gate=sb(f"modG{l}", [128, D]),
            b_mod=Buf(f"mod{l}"),
        )
        cload(sp, cclk, lc["qg"][:], q_norm_g[l].partition_broadcast(128))
        cload(sp, cclk, lc["kg"][:], k_norm_g[l].partition_broadcast(128))
        cload(sp, cclk, lc["sgg"][:], sg_norm_g[l].partition_broadcast(128))
        cload(sp, cclk, lc["hgg"][:], hgrn_norm_g[l].partition_broadcast(128))
        cload(sp, cclk, lc["esink"][:], sinks[l].partition_broadcast(128))
        cload(sp, cclk, lc["bsp"]["p"][:], bsp_p[l])
        cload(sp, cclk, lc["bsp"]["s"][:], bsp_s[l])
        cload(pool, cclk2, lc["wsT"][:], wsT_p[l].rearrange("g s t -> s g t"))
        LC.append(lc)
    lbt = sb("lbt", [128, 512])
    oml = sb("oml", [128, 512])
    lb0, _lb0b = SF.get()
    cload(sp, cclk, lb0[:], hgrn_lb[0].partition_broadcast(128))
    cload(sp, cclk, lbt[:], hgrn_lb[1].partition_broadcast(128))
    dve.op(lambda e: e.tensor_tensor(out=lbt[:], in0=lbt[:], in1=lb0[:], op=ALU.subtract), reads=[b_const], writes=[b_const], partial=True)
    act.op(lambda e: e.activation(out=lbt[:], in_=lbt[:], func=AF.Sigmoid), reads=[b_const], writes=[b_const], partial=True)
    dve.op(lambda e: e.tensor_scalar(out=oml[:], in0=lbt[:], scalar1=-1.0, scalar2=1.0, op0=ALU.mult, op1=ALU.add),
           reads=[b_const], writes=[b_const], partial=True)
    for l in range(DEPTH):
        lc = LC[l]
        act.op(lambda e, lc=lc: e.activation(out=lc["esink"][:], in_=lc["esink"][:], func=AF.Exp),
               reads=[b_const], writes=[b_const], partial=True)
        dve.op(lambda e, lc=lc: e.tensor_tensor(out=lc["wsT"][:], in0=lc["wsT"][:],
                                                in1=bcast(smask["p"][:], [[0, 8], [1, 128]]), op=ALU.mult),
               reads=[b_const], writes=[b_const], partial=True)

    S_p = [sb(f"S_p{l}", [128, 4, 128]) for l in range(DEPTH)]
    Sbf_p = [[sb(f"Sbf_p{l}_{q}", [128, 4, 128], BF16) for q in range(2)] for l in range(DEPTH)]
    b_Sp = [[Buf(f"S_p{l}_{h}") for h in range(4)] for l in range(DEPTH)]
    b_Sbfp = [[[Buf(f"Sbf_p{l}_{q}_{h}") for h in range(4)] for q in range(2)] for l in range(DEPTH)]
    for l in range(DEPTH):
        dve.op(lambda e, l=l: e.memset(S_p[l][:], 0.0), writes=b_Sp[l])
        for q in range(2):
            dve.op(lambda e, l=l, q=q: e.memset(Sbf_p[l][q][:], 0.0), writes=b_Sbfp[l][q])
    NKS = G + 1
    kT_sl = [[sb(f"kT{l}_{i}", [128, 128], BF16) for i in range(NKS)] for l in range(DEPTH)]
    va_sl = [[sb(f"va{l}_{i}", [128, 2, 65], BF16) for i in range(NKS)] for l in range(DEPTH)]
    b_kv = [[Buf(f"kv{l}_{i}") for i in range(NKS)] for l in range(DEPTH)]
    for l in range(DEPTH):
        for i in range(NKS):
            dve.op(lambda e, l=l, i=i: e.memset(va_sl[l][i][:], 1.0), writes=[b_kv[l][i]])
    QgT_m = sb("QgT_m", [128, 4, 8, 128], BF16)
    KtT_m = sb("KtT_m", [128, 4, 8, 128], BF16)
    Kd_m = sb("Kd_m", [128, 8, 512], BF16)
    QtT = sb("QtT", [128, 4, 128], BF16)
    aT_sb = sb("aT_sb", [128, 4, 128], BF16)
    egl_all = sb("egl", [128, G, 4, 16])
    b_QgT, b_KtT, b_Kd, b_QtT, b_aT, b_egl = [Buf(n) for n in ("QgT", "KtT", "Kdm", "QtT", "aT", "egl")]
    dve.op(lambda e: e.memset(QgT_m[:], 0.0), writes=[b_QgT])
    dve.op(lambda e: e.memset(KtT_m[:], 0.0), writes=[b_KtT])

    ARENA_TILE = 19 * 1024
    arena_bytes = max(G * ARENA_TILE, (ARENA_TILE + 31 * 1024) if do_sample else 0)
    arena = sb("arena", [128, arena_bytes // 2], BF16)

    class Carver:
        def __init__(self, pos):
            self.pos = pos

        def take(self, shape, dt=F32):
            n = int(np.prod(shape[1:])) * (4 if dt == F32 else 2)
            assert self.pos + n <= arena_bytes, "arena overflow"
            v = arena[0:shape[0], self.pos // 2:(self.pos + n) // 2]
            if dt == F32:
                v = v.bitcast(F32)
            if len(shape) == 3:
                v = v.rearrange("p (a b) -> p a b", a=shape[1])
            elif len(shape) == 4:
                v = v.rearrange("p (a b c) -> p a b c", a=shape[1], b=shape[2])
            self.pos += (n + 31) // 32 * 32
            return v
    TB = []
    for i in range(G):
        cv_ = Carver(i * ARENA_TILE)
        TB.append(dict(
            x=cv_.take([128, D]), hT=cv_.take([128, 8, 128], BF16), yT=cv_.take([128, 12, 128], BF16),
            qT=cv_.take([128, 4, 128], BF16), sA=cv_.take([128, 512], BF16), sB=cv_.take([128, 512], BF16),
            sC=cv_.take([128, 512], BF16), sgm=cv_.take([128, D], BF16), mg=cv_.take([128, D]),
            b={k: Buf(f"{k}{i}") for k in ("x", "hT", "yT", "qT", "sA", "sB", "sC", "sgm", "mg")},
        ))
    xclk = [fw.dclock(f"x{i}") for i in range(G)]
    yclk = [fw.dclock(f"y{i}") for i in range(G)]
    oclk = RR("osm", 8)
    if do_sample:
        sclk = [fw.dclock(f"smp{i}") for i in range(4)]

    wq = []
    wstate = {"issued": 0, "used": 0}

    def wdesc(kind, l, a=0, n=512):
        if kind == "in":
            return (w_in[l][:, a:a + n].rearrange("(c p) n -> p c n", p=128), 8, n)
        if kind == "ada":
            return (w_ada[l][:, a:a + n].rearrange("(c p) n -> p c n", p=128), 8, n)
        if kind == "br":
            return (w_br[a][l].rearrange("(c p) n -> p c n", p=128), 4, 1024)
        if kind == "out":
            return (w_out[l][:, a:a + n].rearrange("(c p) n -> p c n", p=128), 8, n)
        raise ValueError(kind)

    worder_out = []
    wq_desc = []
    if worder is not None:
        for d_ in worder:
            wq.append(wdesc(*d_))
            wq_desc.append(d_)
    wscr = {}
    wbclk = RR("wb", 4)

    def w_issue():
        i = wstate["issued"]
        if i >= len(wq):
            return
        src, kc, n = wq[i]
        desc = wq_desc[i]
        slot = i % NSLOT
        t, b = WS.items[slot]
        flat = t[:, 0:kc * n]
        if desc in wscr:
            scr, scrb = wscr[desc]
            sp.dma(wclk_hw[slot], flat, scr, reads=[scrb], writes=[b])
        else:
            dst = flat.rearrange("p (c n) -> p c n", c=kc)
            pool.dma(wclk[slot], dst, src, writes=[b])
            scr = nc.dram_tensor("wscr_%d" % len(wscr), [128, kc * n], BF16).ap()
            scrb = Buf("wscr")
            wscr[desc] = (scr, scrb)
            sp.dma(wbclk.get(), scr, flat, reads=[b], writes=[scrb])
        wstate["issued"] += 1

    def w_next(desc):
        i = wstate["used"]
        worder_out.append(desc)
        if worder is None:
            wq.append(wdesc(*desc))
            wq_desc.append(desc)
        else:
            assert worder[i] == desc, (i, worder[i], desc)
        while wstate["issued"] < min(len(wq), i + NSLOT):
            w_issue()
        src, kc, n = wq[i]
        t, b = WS.items[i % NSLOT]
        wstate["used"] += 1
        return t[:, 0:kc * n].rearrange("p (c n) -> p c n", c=kc), b

    def rr(gens):
        gens = list(gens)
        while gens:
            nxt = []
            for g_ in gens:
                try:
                    next(g_)
                    nxt.append(g_)
                except StopIteration:
                    pass
            gens = nxt

    def proj(lhsT_t, lhsT_b, kc, wv, wb, c0, n):
        pt, pb = PS.get()
        for c in range(kc):
            pe.op(lambda e, c=c: e.matmul(pt[:, 0:n], lhsT=lhsT_t[:, c, :], rhs=wv[:, c, c0:c0 + n],
                                          start=(c == 0), stop=(c == kc - 1)),
                  reads=[lhsT_b, wb], writes=[pb], partial=(c > 0), inc=(c == kc - 1))
        return pt, pb

    def transposes(src_t, src_b, nblk, dst_t, dst_b, dst_blk0, full_overwrite=False):
        for j0 in range(0, nblk, 4):
            nb = min(4, nblk - j0)
            pt, pb = PS.get()
            for j in range(nb):
                pe.op(lambda e, j=j, j0=j0: e.matmul(pt[:, j * 128:(j + 1) * 128],
                                                      lhsT=src_t[:, (j0 + j) * 128:(j0 + j + 1) * 128],
                                                      rhs=ident[:], start=True, stop=True),
                      reads=[src_b, b_const], writes=[pb], partial=(j > 0), inc=(j == nb - 1))
            eng = act if (j0 // 4) % 2 == 0 else dve
            dview = dst_t[:, dst_blk0 + j0:dst_blk0 + j0 + nb, :]
            sview = pt[:, 0:nb * 128].rearrange("p (j t) -> p j t", j=nb)
            if eng is act:
                act.op(lambda e, dview=dview, sview=sview: e.copy(out=dview, in_=sview),
                       reads=[pb], writes=[dst_b], partial=not (full_overwrite and j0 == 0 and nb == nblk))
            else:
                dve.op(lambda e, dview=dview, sview=sview: e.tensor_copy(out=dview, in_=sview),
                       reads=[pb], writes=[dst_b], partial=True)

    def rstd_from_ss(ss_t, ss_b, n, width, denom):
        v = ss_t[:, 0:n]
        act.op(lambda e: e.activation(out=v, in_=v, func=AF.Ln, scale=1.0 / denom, bias=epsb[:, 0:1]), reads=[ss_b, b_const], writes=[ss_b])
        act.op(lambda e: e.activation(out=v, in_=v, func=AF.Exp, scale=-0.5), reads=[ss_b], writes=[ss_b])

    def headnorm_rope(pt, pb, c0, nh, gain_t, ropec, ropes):
        n = nh * 64
        sq, sqb = SF.get()
        act.op(lambda e: e.activation(out=sq[:, 0:n], in_=pt[:, c0:c0 + n], func=AF.Square), reads=[pb], writes=[sqb])
        ss, ssb = SM.get()
        dve.op(lambda e: e.tensor_reduce(out=ss[:, 0:nh], in_=sq[:, 0:n].rearrange("p (h d) -> p h d", h=nh),
                                         axis=AX.X, op=ALU.add), reads=[sqb], writes=[ssb])
        rstd_from_ss(ss, ssb, nh, n, HD)
        qn, qb = SF.get()
        q3 = qn[:, 0:n].rearrange("p (h d) -> p h d", h=nh)
        dve.op(lambda e: e.tensor_tensor(out=q3, in0=pt[:, c0:c0 + n].rearrange("p (h d) -> p h d", h=nh),
                                         in1=bcast(ss[:, 0:nh], [[1, nh], [0, HD]]), op=ALU.mult),
               reads=[pb, ssb], writes=[qb])
        dve.op(lambda e: e.tensor_tensor(out=q3, in0=q3, in1=bcast(gain_t[:], [[0, nh], [1, HD]]), op=ALU.mult),
               reads=[qb, b_const], writes=[qb])
        tr, trb = SF.get()
        t3 = tr[:, 0:nh * 16].rearrange("p (h d) -> p h d", h=nh)
        dve.op(lambda e: e.tensor_tensor(out=t3[:, :, 0:8], in0=q3[:, :, 8:16], in1=bcast(ropes[:, 0:8], [[0, nh], [1, 8]]),
                                         op=ALU.mult), reads=[qb, b_const, b_rope], writes=[trb])
        dve.op(lambda e: e.tensor_tensor(out=t3[:, :, 8:16], in0=q3[:, :, 0:8], in1=bcast(ropes[:, 8:16], [[0, nh], [1, 8]]),
                                         op=ALU.mult), reads=[qb, b_const, b_rope], writes=[trb], partial=True)
        dve.op(lambda e: e.tensor_tensor(out=q3[:, :, 0:16], in0=q3[:, :, 0:16], in1=bcast(ropec[:, 0:16], [[0, nh], [1, 16]]),
                                         op=ALU.mult), reads=[qb, trb, b_const, b_rope], writes=[qb])
        dve.op(lambda e: e.tensor_tensor(out=q3[:, :, 0:16], in0=q3[:, :, 0:16], in1=t3, op=ALU.add),
               reads=[qb, trb], writes=[qb])
        return qn, qb

    def compute_mods(c_dram):
        ct, ctb = BIGF.get()
        sp.dma(cclk.get(), ct[:], c_dram, writes=[ctb])
        ch, chb = BIGH.get()
        act.op(lambda e: e.activation(out=ch[:], in_=ct[:], func=AF.Silu), reads=[ctb], writes=[chb])
        cT = TB[0]["hT"]
        cTb = TB[0]["b"]["hT"]
        transposes(ch, chb, 8, cT, cTb, 0, full_overwrite=True)
        for l in range(DEPTH):
            lc = LC[l]
            for jb in range(6):
                wv, wb = w_next(("ada", l, jb * 512, 512))
                pt, pb = proj(cT, cTb, 8, wv, wb, 0, 512)
                bt, bb = SF.get()
                sp.dma(cclk.get(), bt[:], b_ada[l, jb * 512:(jb + 1) * 512].partition_broadcast(128), writes=[bb])
                half = (jb % 2) * 512
                if jb < 2:
                    dst = lc["shift"][:, half:half + 512]
                    dve.op(lambda e, dst=dst, pt=pt, bt=bt: e.tensor_tensor(out=dst, in0=pt[:], in1=bt[:], op=ALU.add),
                           reads=[pb, bb], writes=[lc["b_mod"]], partial=True)
                elif jb < 4:
                    gt, gb_ = SF.get()
                    sp.dma(cclk.get(), gt[:], norm_g[l, half:half + 512].partition_broadcast(128), writes=[gb_])
                    tt, tb_ = SF.get()
                    dve.op(lambda e, pt=pt, bt=bt, tt=tt: e.tensor_tensor(out=tt[:], in0=pt[:], in1=bt[:], op=ALU.add),
                           reads=[pb, bb], writes=[tb_])
                    dst = lc["A"][:, half:half + 512]
                    dve.op(lambda e, dst=dst, tt=tt, gt=gt: e.scalar_tensor_tensor(out=dst, in0=tt[:], scalar=1.0, in1=gt[:],
                                                                                  op0=ALU.add, op1=ALU.mult),
                           reads=[tb_, gb_], writes=[lc["b_mod"]], partial=True)
                else:
                    dst = lc["gate"][:, half:half + 512]
                    dve.op(lambda e, dst=dst, pt=pt, bt=bt: e.tensor_tensor(out=dst, in0=pt[:], in1=bt[:], op=ALU.add),
                           reads=[pb, bb], writes=[lc["b_mod"]], partial=True)

    def mods_blocks():
        for l in range(DEPTH):
            for jb in range(6):
                wq.append(wdesc("ada", l, jb * 512, 512))

    LAYER_BLOCKS = (["qa", "ga", "kv", "ub", "gb", "vb", "qc", "ic", "gc", "fc"]
                    + ["ma0", "ma1", "bra", "mb0", "mb1", "brb", "mc0", "mc1", "brc", "out0", "out1"])

    def layer_blocks(l):
        for name in LAYER_BLOCKS:
            if name == "kv":
                wq.append(wdesc("in", l, OFF["ka"], 256))
            elif name[:2] in ("ma", "mb", "mc"):
                wq.append(wdesc("in", l, OFF[name[:2]] + 512 * int(name[2]), 512))
            elif name.startswith("br"):
                wq.append(wdesc("br", l, "abc".index(name[2])))
            elif name.startswith("out"):
                wq.append(wdesc("out", l, 512 * int(name[3]), 512))
            else:
                wq.append(wdesc("in", l, OFF[name], 512))

    fw.marks = []

    def mark(label):
        fw.marks.append((label, pe.nops))

    def run_group(tiles, kind):
        NC = 8 if kind == "p" else 16
        L = 128 // NC
        if kind == "p":
            t0_, ng_ = tiles[0]["ti"], len(tiles)
            sp.dma(cclk.get(), ropec_p[:, 0:ng_, :], cd["ropec_p"][:, t0_:t0_ + ng_, :], writes=[b_rope])
            sp.dma(cclk.get(), ropes_p[:, 0:ng_, :], cd["ropes_p"][:, t0_:t0_ + ng_, :], writes=[b_rope], partial=True)
        for l in range(DEPTH):
            lc = LC[l]
            last_layer = (l == DEPTH - 1)
            if kind == "s":
                pool.dma(sclk[0], kTc[:], ckT[l].rearrange("b g d j -> (g d) b j"), writes=[b_kTc])
                for g_ in range(2):
                    pool.dma(sclk[1], Vc[:, :, g_, 0:64], cv[l][:, :, 64 * g_:64 * g_ + 64].rearrange("b j d -> j b d"),
                             writes=[b_Vc], partial=True)
                sp.dma(oclk.get(), wks[l][:, 0:120, :], ck[l][:, 8:128, :])
                sp.dma(oclk.get(), wvs[l][:, 0:120, :], cv[l][:, 8:128, :])
            mark("S0")
            def s0_body(tl):
                tb = TB[tl["slot"]]
                B = tb["b"]
                if l == 0:
                    src = xs if kind == "s" else xp[tl["ti"] * 128:(tl["ti"] + 1) * 128, :]
                    sp.dma(xclk[tl["slot"]], tb["x"][:], src, writes=[B["x"]])
                ss, ssb = SM.get()
                hb, hbb = BIGH.get()
                act.op(lambda e, tb=tb, ss=ss, hb=hb: e.activation(out=hb[:], in_=tb["x"][:], func=AF.Square, accum_out=ss[:, 0:1]),
                       reads=[B["x"]], writes=[ssb, hbb])
                rstd_from_ss(ss, ssb, 1, D, D)
                t1, t1b = BIGF.get()
                dve.op(lambda e, tb=tb, ss=ss, t1=t1: e.scalar_tensor_tensor(out=t1[:], in0=tb["x"][:], scalar=ss[:, 0:1],
                                                                            in1=lc["A"][:], op0=ALU.mult, op1=ALU.mult),
                       reads=[B["x"], ssb, lc["b_mod"]], writes=[t1b])
                dve.op(lambda e, t1=t1, hb=hb: e.tensor_tensor(out=hb[:], in0=t1[:], in1=lc["shift"][:], op=ALU.add),
                       reads=[t1b, lc["b_mod"]], writes=[hbb])
                yield
                dbg("h", hb[:], hbb, tl, l, kind)
                transposes(hb, hbb, 8, tb["hT"], B["hT"], 0, full_overwrite=True)
            rr([s0_body(tl) for tl in tiles])

            mark("qa")
            wv, wb = w_next(("in", l, OFF["qa"], 512))

            def qa_body(tl):
                tb = TB[tl["slot"]]
                B = tb["b"]
                rc = ropec_s[:, 0, :] if kind == "s" else ropec_p[:, tl["slot"], :]
                rs = ropes_s[:, 0, :] if kind == "s" else ropes_p[:, tl["slot"], :]
                pt, pb = proj(tb["hT"], B["hT"], 8, wv, wb, 0, 512)
                yield
                qn, qb = headnorm_rope(pt, pb, 0, 8, lc["qg"], rc, rs)
                dbg("q", qn[:], qb, tl, l, kind)
                qh, qhb = SH.get()
                dve.op(lambda e, qh=qh, qn=qn: e.tensor_copy(
                    out=qh[:].rearrange("p (j g d) -> p g j d", j=4, g=2),
                    in_=qn[:].rearrange("p (g j d) -> p g j d", g=2, j=4)), reads=[qb], writes=[qhb])
                yield
                transposes(qh, qhb, 4, tb["qT"], B["qT"], 0, full_overwrite=True)
            rr([qa_body(tl) for tl in tiles])
            mark("ga")
            wv, wb = w_next(("in", l, OFF["ga"], 512))

            def ga_body(tl):
                tb = TB[tl["slot"]]
                B = tb["b"]
                pt, pb = proj(tb["hT"], B["hT"], 8, wv, wb, 0, 512)
                yield
                act.op(lambda e, tb=tb, pt=pt: e.activation(out=tb["sA"][:], in_=pt[:], func=AF.Silu), reads=[pb], writes=[B["sA"]])
            rr([ga_body(tl) for tl in tiles])
            mark("kv")
            wv, wb = w_next(("in", l, OFF["ka"], 256))

            def kv_body(idx, tl):
                tb = TB[tl["slot"]]
                B = tb["b"]
                rc = ropec_s[:, 0, :] if kind == "s" else ropec_p[:, tl["slot"], :]
                rs = ropes_s[:, 0, :] if kind == "s" else ropes_p[:, tl["slot"], :]
                pt, pb = proj(tb["hT"], B["hT"], 8, wv, wb, 0, 256)
                yield
                kn, knb = headnorm_rope(pt, pb, 0, 2, lc["kg"], rc, rs)
                cur = idx + 1
                kT_c, va_c, bkv_c = kT_sl[l][cur], va_sl[l][cur], b_kv[l][cur]
                kh, khb = SH.get()
                dve.op(lambda e, kh=kh, kn=kn: e.tensor_copy(out=kh[:, 0:128], in_=kn[:, 0:128]), reads=[knb], writes=[khb])
                dve.op(lambda e, pt=pt, va_c=va_c: e.tensor_copy(out=va_c[:, :, 0:64],
                                                                 in_=pt[:, 128:256].rearrange("p (g d) -> p g d", g=2)),
                       reads=[pb], writes=[bkv_c], partial=True)
                want_out = (kind == "s") or (tl["ti"] == nt - 1)
                if want_out:
                    vf, vfb = SF.get()
                    act.op(lambda e, vf=vf, pt=pt: e.copy(out=vf[:, 0:128], in_=pt[:, 128:256]), reads=[pb], writes=[vfb])
                yield
                p2, p2b = PS.get()
                pe.op(lambda e, p2=p2, kh=kh: e.matmul(p2[:, 0:128], lhsT=kh[:, 0:128], rhs=ident[:], start=True, stop=True),
                      reads=[khb, b_const], writes=[p2b])
                act.op(lambda e, p2=p2, kT_c=kT_c: e.copy(out=kT_c[:], in_=p2[:, 0:128]), reads=[p2b], writes=[bkv_c], partial=True)
                yield
                if want_out:
                    if kind == "p":
                        sp.dma(oclk.get(), wkp[l], kn[:, 0:128], reads=[knb])
                        sp.dma(oclk.get(), wvp[l], vf[:, 0:128], reads=[vfb])
                    else:
                        for b in range(16):
                            sp.dma(oclk.get(), wks[l][b, 120:128, :], kn[8 * b:8 * b + 8, 0:128], reads=[knb])
                            sp.dma(oclk.get(), wvs[l][b, 120:128, :], vf[8 * b:8 * b + 8, 0:128], reads=[vfb])
                ya, yab = SF.get()
                pairs = []
                accs = []
                for g in range(2):
                    if kind == "p":
                        kts = []
                        if tl["ti"] > 0:
                            kts.append((kT_sl[l][cur - 1], va_sl[l][cur - 1][:, g, :], b_kv[l][cur - 1], amask_p[:, 0, :]))
                        kts.append((kT_c, va_c[:, g, :], bkv_c, amask_p[:, 1, :]))
                        kts = [(k_[64 * g:64 * g + 64, :], v_, [b_], m_) for (k_, v_, b_, m_) in kts]
                    else:
                        kts = [(kTc[64 * g:64 * g + 64, b, :], Vc[:, b, g, :], [b_kTc, b_Vc], amask_s[:, b, :]) for b in range(16)]
                        kts.append((kT_c[64 * g:64 * g + 64, :], va_c[:, g, :], [bkv_c], amask_s[:, 16, :]))
                    ot, ob = PSO.get()
                    o3 = ot[:, 0:260].rearrange("p (j d) -> p j d", j=4)
                    accs.append((ot, ob, o3))
                    for ki, kt in enumerate(kts):
                        pairs.append((g, ki, len(kts), kt))
                q_b = B["qT"]
                for p0 in range(0, len(pairs), 4):
                    batch = pairs[p0:p0 + 4]
                    work = []
                    for (g, ki, nk, (k_ap, v_ap, kvbs, m_ap)) in batch:
                        sc, scb = PS.get()
                        q_ap = tb["qT"][64 * g:64 * g + 64, :, :]
                        pe.op(lambda e, sc=sc, k_ap=k_ap, q_ap=q_ap: e.matmul(sc[:], lhsT=k_ap, rhs=q_ap, start=True, stop=True),
                              reads=[q_b] + kvbs, writes=[scb])
                        work.append((sc, scb))
                    pts = []
                    for (sc, scb) in work:
                        ptt, ptb = SH.get()
                        act.op(lambda e, sc=sc, ptt=ptt: e.activation(out=ptt[:], in_=sc[:], func=AF.Exp, scale=HD ** -0.5),
                               reads=[scb], writes=[ptb])
                        pts.append((ptt, ptb))
                    for (ptt, ptb), (g, ki, nk, (k_ap, v_ap, kvbs, m_ap)) in zip(pts, batch):
                        dve.op(lambda e, ptt=ptt, m_ap=m_ap: e.tensor_tensor(
                            out=ptt[:].rearrange("p (j q) -> p j q", j=4), in0=ptt[:].rearrange("p (j q) -> p j q", j=4),
                            in1=bcast(m_ap, [[0, 4], [1, 128]]), op=ALU.mult), reads=[ptb, b_const], writes=[ptb])
                    for (ptt, ptb), (g, ki, nk, (k_ap, v_ap, kvbs, m_ap)) in zip(pts, batch):
                        ot, ob, o3 = accs[g]
                        for j in range(4):
                            pe.op(lambda e, j=j, o3=o3, ptt=ptt, v_ap=v_ap, ki=ki, nk=nk: e.matmul(
                                o3[:, j, :], lhsT=ptt[:, j * 128:(j + 1) * 128], rhs=v_ap,
                                start=(ki == 0 and j == 0), stop=(ki == nk - 1), skip_group_check=True),
                                reads=[ptb] + kvbs, writes=[ob], partial=not (ki == 0 and j == 0), inc=(j == 3))
                for g in range(2):
                    ot, ob, o3 = accs[g]
                    den, denb = SM.get()
                    dve.op(lambda e, den=den, o3=o3, g=g: e.tensor_tensor(out=den[:, 0:4], in0=o3[:, :, 64],
                                                                          in1=lc["esink"][:, 4 * g:4 * g + 4], op=ALU.add),
                           reads=[ob, b_const], writes=[denb])
                    dve.op(lambda e, den=den: e.reciprocal(out=den[:, 0:4], in_=den[:, 0:4]), reads=[denb], writes=[denb])
                    dve.op(lambda e, ya=ya, o3=o3, den=den, g=g: e.tensor_tensor(
                        out=ya[:, 256 * g:256 * g + 256].rearrange("p (j d) -> p j d", j=4), in0=o3[:, :, 0:64],
                        in1=bcast(den[:, 0:4], [[1, 4], [0, 64]]), op=ALU.mult),
                        reads=[ob, denb], writes=[yab], partial=(g == 1))
                dbg("k", kn[:, 0:128], knb, tl, l, kind)
                dbg("ya", ya[:], yab, tl, l, kind)
                yield
                yg, ygb = SH.get()
                dve.op(lambda e, yg=yg, ya=ya, tb=tb: e.tensor_tensor(out=yg[:], in0=ya[:], in1=tb["sA"][:], op=ALU.mult),
                       reads=[yab, B["sA"]], writes=[ygb])
                transposes(yg, ygb, 4, tb["yT"], B["yT"], 0)
            rr([kv_body(i_, tl) for i_, tl in enumerate(tiles)])
            if kind == "p":
                last = len(tiles)
                act.op(lambda e: e.copy(out=kT_sl[l][0][:], in_=kT_sl[l][last][:]), reads=[b_kv[l][last]], writes=[b_kv[l][0]], partial=True)
                dve.op(lambda e: e.tensor_copy(out=va_sl[l][0][:], in_=va_sl[l][last][:]), reads=[b_kv[l][last]], writes=[b_kv[l][0]], partial=True)
            mark("ub")
            wv, wb = w_next(("in", l, OFF["ub"], 512))

            def ub_body(tl):
                tb = TB[tl["slot"]]
                B = tb["b"]
                pt, pb = proj(tb["hT"], B["hT"], 8, wv, wb, 0, 512)
                yield
                act.op(lambda e, tb=tb, pt=pt: e.copy(out=tb["sB"][:], in_=pt[:]), reads=[pb], writes=[B["sB"]])
            rr([ub_body(tl) for tl in tiles])
            mark("gb")
            wv, wb = w_next(("in", l, OFF["gb"], 512))

            def gb_body(tl):
                tb = TB[tl["slot"]]
                B = tb["b"]
                pt, pb = proj(tb["hT"], B["hT"], 8, wv, wb, 0, 512)
                yield
                act.op(lambda e, tb=tb, pt=pt: e.activation(out=tb["sC"][:], in_=pt[:], func=AF.Silu), reads=[pb], writes=[B["sC"]])
                pool.op(lambda e, tb=tb: e.tensor_tensor(out=tb["sC"][:], in0=tb["sC"][:], in1=tb["sB"][:], op=ALU.mult),
                        reads=[B["sC"], B["sB"]], writes=[B["sC"]])
            rr([gb_body(tl) for tl in tiles])
            mark("vb")
            wv, wb = w_next(("in", l, OFF["vb"], 512))

            def vb_body(tl):
                tb = TB[tl["slot"]]
                B = tb["b"]
                pt, pb = proj(tb["hT"], B["hT"], 8, wv, wb, 0, 512)
                yield
                ss, ssb = SM.get()
                vh, vhb = SH.get()
                act.op(lambda e, pt=pt, ss=ss, vh=vh: e.activation(out=vh[:], in_=pt[:], func=AF.Square, accum_out=ss[:, 0:1]),
                       reads=[pb], writes=[ssb, vhb])
                rstd_from_ss(ss, ssb, 1, 512, 512)
                vn, vnb = SF.get()
                dve.op(lambda e, vn=vn, pt=pt, ss=ss: e.scalar_tensor_tensor(out=vn[:], in0=pt[:], scalar=ss[:, 0:1], in1=lc["sgg"][:],
                                                                            op0=ALU.mult, op1=ALU.mult),
                       reads=[pb, ssb, b_const], writes=[vnb])
                if kind == "s":
                    sp.dma(oclk.get(), sgv[l], vn[:], reads=[vnb])
                act.op(lambda e, vh=vh, vn=vn: e.copy(out=vh[:], in_=vn[:]), reads=[vnb], writes=[vhb])
                yield
                zt, zb = PS.get()
                for g8 in range(8):
                    pe.op(lambda e, g8=g8, zt=zt, vh=vh: e.matmul(zt[:, g8 * 64:(g8 + 1) * 64], lhsT=lc["wsT"][:, g8, :],
                                                                  rhs=vh[:, g8 * 64:(g8 + 1) * 64], start=True, stop=True),
                          reads=[vhb, b_const], writes=[zb], partial=(g8 > 0), inc=(g8 == 7))
                yield
                yb_, ybb = SF.get()
                dve.op(lambda e, yb_=yb_, zt=zt: e.tensor_tensor(out=yb_[:].rearrange("p (g d) -> p g d", g=8),
                                                                 in0=zt[:].rearrange("p (g d) -> p g d", g=8),
                                                                 in1=bcast(lc["bsp"][kind][:], [[1, 8], [0, 64]]), op=ALU.add),
                       reads=[zb, b_const], writes=[ybb])
                yg, ygb = SH.get()
                dve.op(lambda e, yg=yg, yb_=yb_, tb=tb: e.tensor_tensor(out=yg[:], in0=yb_[:], in1=tb["sC"][:], op=ALU.mult),
                       reads=[ybb, B["sC"]], writes=[ygb])
                dbg("ybg", yg[:], ygb, tl, l, kind)
                yield
                transposes(yg, ygb, 4, tb["yT"], B["yT"], 4)
            rr([vb_body(tl) for tl in tiles])
            mark("qcicgc")
            for nm in ("qc", "ic", "gc"):
                wv, wb = w_next(("in", l, OFF[nm], 512))

                def c3_body(tl, nm=nm, wv=wv, wb=wb):
                    tb = TB[tl["slot"]]
                    B = tb["b"]
                    pt, pb = proj(tb["hT"], B["hT"], 8, wv, wb, 0, 512)
                    yield
                    if nm == "qc":
                        act.op(lambda e, tb=tb, pt=pt: e.mul(out=tb["sA"][:], in_=pt[:], mul=DK ** -0.5), reads=[pb], writes=[B["sA"]])
                    elif nm == "ic":
                        act.op(lambda e, tb=tb, pt=pt: e.copy(out=tb["sB"][:], in_=pt[:]), reads=[pb], writes=[B["sB"]])
                    else:
                        act.op(lambda e, tb=tb, pt=pt: e.activation(out=tb["sC"][:], in_=pt[:], func=AF.Silu),
                               reads=[pb], writes=[B["sC"]])
                        pool.op(lambda e, tb=tb: e.tensor_tensor(out=tb["sC"][:].rearrange("p (h d) -> p h d", h=4),
                                                                 in0=tb["sC"][:].rearrange("p (h d) -> p h d", h=4),
                                                                 in1=bcast(lc["hgg"][:], [[0, 4], [1, 128]]), op=ALU.mult),
                                reads=[B["sC"], b_const], writes=[B["sC"]])
                rr([c3_body(tl) for tl in tiles])
            mark("fc")
            wv, wb = w_next(("in", l, OFF["fc"], 512))

            def fc_body(tl):
                tb = TB[tl["slot"]]
                B = tb["b"]
                pt, pb = proj(tb["hT"], B["hT"], 8, wv, wb, 0, 512)
                yield
                ft = tb["sgm"].bitcast(F32)
                fb = B["sgm"]
                act.op(lambda e, ft=ft, pt=pt: e.activation(out=ft[:], in_=pt[:], func=AF.Sigmoid), reads=[pb], writes=[fb])
                if l > 0:
                    dve.op(lambda e, ft=ft: e.tensor_tensor(out=ft[:], in0=ft[:], in1=oml[:], op=ALU.mult), reads=[fb, b_const], writes=[fb])
                    dve.op(lambda e, ft=ft: e.scalar_tensor_tensor(out=ft[:], in0=ft[:], scalar=TINY, in1=lbt[:], op0=ALU.max, op1=ALU.add),
                           reads=[fb, b_const], writes=[fb])
                else:
                    dve.op(lambda e, ft=ft: e.tensor_scalar_max(out=ft[:], in0=ft[:], scalar1=TINY), reads=[fb], writes=[fb])
                lf = tb["mg"][:, 512:1024]
                lfb = B["mg"]
                act.op(lambda e, lf=lf, ft=ft: e.activation(out=lf, in_=ft[:], func=AF.Ln), reads=[fb], writes=[lfb], partial=True)
                kk = ft
                kkb = fb
                dve.op(lambda e, kk=kk, ft=ft: e.tensor_scalar(out=kk[:], in0=ft[:], scalar1=-1.0, scalar2=1.0, op0=ALU.mult, op1=ALU.add),
                       reads=[fb], writes=[kkb])
                yield
                egl = egl_all[:, tl["slot"], :, :]
                b_egl = tb["b"].setdefault("egl", Buf("egl"))
                gps = []
                for m in range(3):
                    gt_, gb_ = PS.get()
                    pe.op(lambda e, gt_=gt_, m=m, lf=lf: e.matmul(gt_[:], lhsT=hm[kind][:, m, :], rhs=lf, start=True, stop=True),
                          reads=[lfb, b_const], writes=[gb_])
                    gps.append((gt_, gb_))
                glt, glb = PS.get()
                for h in range(4):
                    pe.op(lambda e, h=h, glt=glt, lf=lf: e.matmul(glt[:, h * 16:h * 16 + NC], lhsT=lf[:, h * 128:(h + 1) * 128],
                                                                  rhs=csel[kind][:], start=True, stop=True),
                          reads=[lfb, b_const], writes=[glb], partial=(h > 0), inc=(h == 3))
                act.op(lambda e, glt=glt: e.activation(out=egl[:, :, 0:NC], in_=glt[:, 0:64].rearrange("p (h c) -> p h c", h=4)[:, :, 0:NC],
                                                       func=AF.Exp), reads=[glb], writes=[b_egl])
                exps = []
                for (m, sc_) in ((0, 1.0), (1, 1.0), (1, -1.0), (2, 1.0)):
                    et, eb = SF.get()
                    act.op(lambda e, et=et, m=m, sc_=sc_: e.activation(out=et[:], in_=gps[m][0][:], func=AF.Exp, scale=sc_),
                           reads=[gps[m][1]], writes=[eb])
                    exps.append((et, eb))
                Qt = tb["qT"][:].rearrange("p a b -> p (a b)")
                Qtb = B["qT"]
                dve.op(lambda e, Qt=Qt, tb=tb: e.tensor_tensor(out=Qt, in0=tb["sA"][:], in1=exps[1][0][:], op=ALU.mult),
                       reads=[B["sA"], exps[1][1]], writes=[Qtb])
                Qg = tb["sA"]
                Qgb = B["sA"]
                dve.op(lambda e, Qg=Qg, tb=tb: e.tensor_tensor(out=Qg[:], in0=tb["sA"][:], in1=exps[0][0][:], op=ALU.mult),
                       reads=[B["sA"], exps[0][1]], writes=[Qgb])
                Kt = tb["yT"][:, 8:12, :].rearrange("p a b -> p (a b)")
                Ktb = B["yT"]
                dve.op(lambda e, Kt=Kt, kk=kk: e.tensor_tensor(out=Kt, in0=kk[:], in1=exps[2][0][:], op=ALU.mult),
                       reads=[kkb, exps[2][1]], writes=[Ktb], partial=True)
                Kd = tb["mg"][:, 512:768].bitcast(BF16)
                Kdb = B["mg"]
                dve.op(lambda e, Kd=Kd, kk=kk: e.tensor_tensor(out=Kd, in0=kk[:], in1=exps[3][0][:], op=ALU.mult),
                       reads=[kkb, exps[3][1]], writes=[Kdb], partial=True)
                iv = tb["sB"]
                ivb = B["sB"]
                yield
                tps = []
                for (src, srcb) in ((Qg, Qgb), (Qt, Qtb), (Kt, Ktb)):
                    tp_, tpb = PS.get()
                    for j in range(4):
                        pe.op(lambda e, j=j, tp_=tp_, src=src: e.matmul(tp_[:, j * 128:(j + 1) * 128], lhsT=src[:, j * 128:(j + 1) * 128],
                                                                        rhs=ident[:], start=True, stop=True),
                              reads=[srcb, b_const], writes=[tpb], partial=(j > 0), inc=(j == 3))
                    tps.append((tp_, tpb))
                dve.op(lambda e: e.tensor_copy(out=QtT[:], in_=tps[1][0][:].rearrange("p (h t) -> p h t", h=4)),
                       reads=[tps[1][1]], writes=[b_QtT])
                if kind == "p":
                    def bd(dst):
                        return AP(dst[:].tensor, dst[:].offset, [list(dst[:].ap[0]), [NC * 128, 4], [128 + L, NC], [1, L]])
                    act.op(lambda e: e.copy(out=bd(QgT_m), in_=tps[0][0][:].rearrange("p (h c j) -> p h c j", h=4, c=NC)),
                           reads=[tps[0][1]], writes=[b_QgT], partial=True)
                    act.op(lambda e: e.copy(out=bd(KtT_m), in_=tps[2][0][:].rearrange("p (h c j) -> p h c j", h=4, c=NC)),
                           reads=[tps[2][1]], writes=[b_KtT], partial=True)
                    for c in range(NC):
                        dve.op(lambda e, Kd=Kd, c=c: e.tensor_scalar_mul(out=Kd_m[:, c, :], in0=Kd, scalar1=cselh[kind][:, c:c + 1]),
                               reads=[Kdb, b_const], writes=[b_Kd], partial=(c > 0))
                else:
                    QgT_s = sb_s["QgT_s"]
                    KtT_s = sb_s["KtT_s"]
                    act.op(lambda e: e.copy(out=QgT_s[:], in_=tps[0][0][:].rearrange("p (h t) -> p h t", h=4)),
                           reads=[tps[0][1]], writes=[sb_s["b_QgT_s"]])
                    act.op(lambda e: e.copy(out=KtT_s[:], in_=tps[2][0][:].rearrange("p (h t) -> p h t", h=4)),
                           reads=[tps[2][1]], writes=[sb_s["b_KtT_s"]])
                at_, atb = PS.get()
                a3 = at_[:].rearrange("p (h t) -> p h t", h=4)
                if kind == "p":
                    for h in range(4):
                        for c in range(NC):
                            pe.op(lambda e, h=h, c=c: e.matmul(a3[:, h, c * L:(c + 1) * L], lhsT=KtT_m[:, h, c, :],
                                                               rhs=QtT[:, h, c * L:(c + 1) * L], start=True, stop=True),
                                  reads=[b_KtT, b_QtT], writes=[atb], partial=not (h == 0 and c == 0),
                                  inc=(h == 3 and c == NC - 1))
                else:
                    for h in range(4):
                        pe.op(lambda e, h=h: e.matmul(a3[:, h, :], lhsT=sb_s["KtT_s"][:, h, :], rhs=QtT[:, h, :], start=True, stop=True),
                              reads=[sb_s["b_KtT_s"], b_QtT], writes=[atb], partial=(h > 0), inc=(h == 3))
                dve.op(lambda e: e.scalar_tensor_tensor(out=aT_sb[:], in0=a3, scalar=1e30, in1=bcast(cmask[kind][:], [[0, 4], [1, 128]]),
                                                        op0=ALU.min, op1=ALU.mult), reads=[atb, b_const], writes=[b_aT])
                ot, ob = PSO.get()
                o3 = ot[:].rearrange("p (h v) -> p h v", h=4)
                if kind == "p":
                    for c0 in range(0, NC, 4):
                        sus = []
                        for c in range(c0, c0 + 4):
                            su, sub = PS.get()
                            s3 = su[:].rearrange("p (h v) -> p h v", h=4)
                            for h in range(4):
                                pe.op(lambda e, h=h, c=c, s3=s3: e.matmul(s3[:, h, :], lhsT=Kd_m[:, c, h * 128:(h + 1) * 128],
                                                                          rhs=iv[:, h * 128:(h + 1) * 128], start=True, stop=True),
                                      reads=[b_Kd, ivb], writes=[sub], partial=(h > 0), inc=(h == 3))
                            sus.append((s3, sub))
                        for c in range(c0, c0 + 4):
                            s3, sub = sus[c - c0]
                            q_ = c % 2
                            for h in range(4):
                                pe.op(lambda e, h=h, c=c, q_=q_: e.matmul(o3[:, h, :], lhsT=QgT_m[:, h, c, :], rhs=Sbf_p[l][q_][:, h, :],
                                                                          start=(c == 0 and h == 0), stop=False, skip_group_check=True),
                                      reads=[b_QgT, b_Sbfp[l][q_][h]], writes=[ob], partial=not (c == 0 and h == 0))
                            for h in range(4):
                                dve.op(lambda e, h=h, c=c, s3=s3: e.scalar_tensor_tensor(
                                    out=S_p[l][:, h, :], in0=S_p[l][:, h, :], scalar=egl[:, h, c:c + 1], in1=s3[:, h, :],
                                    op0=ALU.mult, op1=ALU.add), reads=[b_Sp[l][h], b_egl, sub], writes=[b_Sp[l][h]])
                                act.op(lambda e, h=h, q_=q_: e.copy(out=Sbf_p[l][1 - q_][:, h, :], in_=S_p[l][:, h, :]),
                                       reads=[b_Sp[l][h]], writes=[b_Sbfp[l][1 - q_][h]])
                    for h in range(4):
                        pe.op(lambda e, h=h: e.matmul(o3[:, h, :], lhsT=aT_sb[:, h, :], rhs=iv[:, h * 128:(h + 1) * 128],
                                                      start=False, stop=True, skip_group_check=True),
                              reads=[b_aT, ivb], writes=[ob], partial=True, inc=(h == 3))
                    if tl["ti"] == nt - 1:
                        sp.dma(oclk.get(), hgp[l].rearrange("h k v -> k h v"), S_p[l][:], reads=b_Sp[l])
                else:
                    for h in range(4):
                        sp.dma(sclk[2], S_s[:], sth[l][:, h, :, :].rearrange("b k v -> k b v"), writes=[b_Ss])
                        pool.dma(sclk[3], Sbf_s[:], sth[l][:, h, :, :].rearrange("b k v -> k b v"), writes=[b_Sbfs])
                        bd16 = AP(QgT_ms[:].tensor, QgT_ms[:].offset, [list(QgT_ms[:].ap[0]), [128 + 8, 16], [1, 8]])
                        act.op(lambda e, h=h, bd16=bd16: e.copy(out=bd16, in_=sb_s["QgT_s"][:, h, :].rearrange("p (c j) -> p c j", c=16)),
                               reads=[sb_s["b_QgT_s"]], writes=[b_QgTs], partial=True)
                        dve.op(lambda e, h=h, Kd=Kd: e.tensor_tensor(out=Kd_ms[:], in0=bcast(Kd[:, h * 128:(h + 1) * 128], [[0, 16], [1, 128]]),
                                                                     in1=bcast(cselh["s"][:], [[1, 16], [0, 128]]), op=ALU.mult),
                               reads=[Kdb, b_const], writes=[b_Kds])
                        for b in range(16):
                            pe.op(lambda e, h=h, b=b: e.matmul(o3[:, h, :], lhsT=QgT_ms[:, b, :], rhs=Sbf_s[:, b, :],
                                                               start=(b == 0 and h == 0), stop=False, skip_group_check=True),
                                  reads=[b_QgTs, b_Sbfs], writes=[ob], partial=not (h == 0 and b == 0), inc=(b == 15))
                        pe.op(lambda e, h=h: e.matmul(o3[:, h, :], lhsT=aT_sb[:, h, :], rhs=iv[:, h * 128:(h + 1) * 128],
                                                      start=False, stop=True, skip_group_check=True), reads=[b_aT, ivb], writes=[ob], partial=True)
                        for b4 in range(0, 16, 4):
                            su, sub = PS.get()
                            s3 = su[:].rearrange("p (b v) -> p b v", b=4)
                            for bb in range(4):
                                pe.op(lambda e, h=h, bb=bb, b4=b4, s3=s3: e.matmul(s3[:, bb, :], lhsT=Kd_ms[:, b4 + bb, :],
                                                                                   rhs=iv[:, h * 128:(h + 1) * 128], start=True, stop=True),
                                      reads=[b_Kds, ivb], writes=[sub], partial=(bb > 0), inc=(bb == 3))
                            for bb in range(4):
                                dve.op(lambda e, h=h, bb=bb, b4=b4, s3=s3: e.scalar_tensor_tensor(
                                    out=S_s[:, b4 + bb, :], in0=S_s[:, b4 + bb, :], scalar=egl[:, h, b4 + bb:b4 + bb + 1],
                                    in1=s3[:, bb, :], op0=ALU.mult, op1=ALU.add), reads=[b_Ss, b_egl, sub], writes=[b_Ss], partial=True)
                        sp.dma(oclk.get(), hgs[l][:, h, :, :].rearrange("b k v -> k b v"), S_s[:], reads=[b_Ss])
                oc = tb["mg"][:, 0:512]
                act.op(lambda e, oc=oc, ot=ot: e.copy(out=oc, in_=ot[:]), reads=[ob], writes=[B["mg"]], partial=True)
                yield
                ob = B["mg"]
                o3 = oc.rearrange("p (h v) -> p h v", h=4)
                sq, sqb = SF.get()
                act.op(lambda e, sq=sq, oc=oc: e.activation(out=sq[:], in_=oc, func=AF.Square), reads=[ob], writes=[sqb])
                ss, ssb = SM.get()
                dve.op(lambda e, ss=ss, sq=sq: e.tensor_reduce(out=ss[:, 0:4], in_=sq[:].rearrange("p (h d) -> p h d", h=4),
                                                               axis=AX.X, op=ALU.add), reads=[sqb], writes=[ssb])
                rstd_from_ss(ss, ssb, 4, 512, DK)
                yc, ycb = SF.get()
                dve.op(lambda e, yc=yc, o3=o3, ss=ss: e.tensor_tensor(out=yc[:].rearrange("p (h d) -> p h d", h=4), in0=o3,
                                                                      in1=bcast(ss[:, 0:4], [[1, 4], [0, 128]]), op=ALU.mult),
                       reads=[ob, ssb], writes=[ycb])
                dbg("yc", yc[:], ycb, tl, l, kind)
                dbg("lf", lf, lfb, tl, l, kind)
                yg, ygb = SH.get()
                dve.op(lambda e, yg=yg, yc=yc, tb=tb: e.tensor_tensor(out=yg[:], in0=yc[:], in1=tb["sC"][:], op=ALU.mult),
                       reads=[ycb, B["sC"]], writes=[ygb])
                transposes(yg, ygb, 4, tb["yT"], B["yT"], 8)
            rr([fc_body(tl) for tl in tiles])
            mark("merge")
            for br in range(3):
                for half in range(2):
                    wv, wb = w_next(("in", l, OFF[("ma", "mb", "mc")[br]] + 512 * half, 512))

                    def mg_body(tl, half=half, wv=wv, wb=wb):
                        tb = TB[tl["slot"]]
                        B = tb["b"]
                        pt, pb = proj(tb["hT"], B["hT"], 8, wv, wb, 0, 512)
                        yield
                        act.op(lambda e, tb=tb, pt=pt, half=half: e.activation(out=tb["sgm"][:, half * 512:(half + 1) * 512], in_=pt[:],
                                                                               func=AF.Sigmoid), reads=[pb], writes=[B["sgm"]], partial=(half == 1))
                    rr([mg_body(tl) for tl in tiles])
                wv, wb = w_next(("br", l, br, 0))

                def br_body(tl, br=br, wv=wv, wb=wb):
                    tb = TB[tl["slot"]]
                    B = tb["b"]
                    pps = []
                    for half in range(2):
                        pt, pb = PS.get()
                        for c in range(4):
                            pe.op(lambda e, c=c, pt=pt, tb=tb, half=half, br=br: e.matmul(
                                pt[:], lhsT=tb["yT"][:, 4 * br + c, :], rhs=wv[:, c, half * 512:(half + 1) * 512],
                                start=(c == 0), stop=(c == 3)), reads=[B["yT"], wb], writes=[pb], partial=(c > 0), inc=(c == 3))
                        pps.append((pt, pb))
                    yield
                    for half in range(2):
                        pt, pb = pps[half]
                        mg_h = tb["mg"][:, half * 512:(half + 1) * 512]
                        sg_h = tb["sgm"][:, half * 512:(half + 1) * 512]
                        if br == 0:
                            dve.op(lambda e, mg_h=mg_h, pt=pt, sg_h=sg_h: e.tensor_tensor(out=mg_h, in0=pt[:], in1=sg_h, op=ALU.mult),
                                   reads=[pb, B["sgm"]], writes=[B["mg"]], partial=True)
                        else:
                            tt, ttb = SF.get()
                            dve.op(lambda e, tt=tt, pt=pt, sg_h=sg_h: e.tensor_tensor(out=tt[:], in0=pt[:], in1=sg_h, op=ALU.mult),
                                   reads=[pb, B["sgm"]], writes=[ttb])
                            pool.op(lambda e, mg_h=mg_h, tt=tt: e.tensor_tensor(out=mg_h, in0=mg_h, in1=tt[:], op=ALU.add),
                                    reads=[ttb, B["mg"]], writes=[B["mg"]], partial=True)
                    if br == 2:
                        dbg("mg", tb["mg"][:], B["mg"], tl, l, kind)
                        mh, mhb = BIGH.get()
                        act.op(lambda e, mh=mh, tb=tb: e.copy(out=mh[:], in_=tb["mg"][:]), reads=[B["mg"]], writes=[mhb])
                        yield
                        transposes(mh, mhb, 8, tb["hT"], B["hT"], 0, full_overwrite=True)
                rr([br_body(tl) for tl in tiles])
            mark("out")
            for half in range(2):
                wv, wb = w_next(("out", l, 512 * half, 512))

                def out_body(tl, half=half, wv=wv, wb=wb):
                    tb = TB[tl["slot"]]
                    B = tb["b"]
                    pt, pb = proj(tb["hT"], B["hT"], 8, wv, wb, 0, 512)
                    yield
                    tt, ttb = SF.get()
                    dve.op(lambda e, tt=tt, pt=pt, half=half: e.tensor_tensor(out=tt[:], in0=pt[:], in1=lc["gate"][:, half * 512:(half + 1) * 512],
                                                                             op=ALU.mult), reads=[pb, lc["b_mod"]], writes=[ttb])
                    xh = tb["x"][:, half * 512:(half + 1) * 512]
                    dve.op(lambda e, xh=xh, tt=tt: e.tensor_tensor(out=xh, in0=xh, in1=tt[:], op=ALU.add),
                           reads=[ttb, B["x"]], writes=[B["x"]], partial=True)
                    if half == 1:
                        dbg("xo", tb["x"][:], B["x"], tl, l, kind)
                    if last_layer and half == 1:
                        dst = ys if kind == "s" else yp[tl["ti"] * 128:(tl["ti"] + 1) * 128, :]
                        sp.dma(yclk[tl["slot"]], dst, tb["x"][:], reads=[B["x"]])
                rr([out_body(tl) for tl in tiles])

    groups = []
    t = 0
    while t < nt:
        g = list(range(t, min(nt, t + G)))
        groups.append(g)
        t += G
    sb_s = {}
    compute_mods(cp)
    for g in groups:
        run_group([dict(ti=ti, slot=i) for i, ti in enumerate(g)], "p")
    if do_sample:
        fw.barrier()
        cv_ = Carver(ARENA_TILE)
        kTc = cv_.take([128, 16, 128], BF16)
        Vc = cv_.take([128, 16, 2, 65], BF16)
        S_s = cv_.take([128, 16, 128])
        Sbf_s = cv_.take([128, 16, 128], BF16)
        QgT_ms = cv_.take([128, 16, 128], BF16)
        Kd_ms = cv_.take([128, 16, 128], BF16)
        sb_s["QgT_s"] = cv_.take([128, 4, 128], BF16)
        sb_s["KtT_s"] = cv_.take([128, 4, 128], BF16)
        sb_s["b_QgT_s"] = Buf("QgT_s")
        sb_s["b_KtT_s"] = Buf("KtT_s")
        b_kTc, b_Vc, b_Ss, b_Sbfs, b_QgTs, b_Kds = [Buf(n) for n in ("kTc", "Vc", "Ss", "Sbfs", "QgTs", "Kds")]
        dve.op(lambda e: e.memset(Vc[:], 1.0), writes=[b_Vc])
        dve.op(lambda e: e.memset(QgT_ms[:], 0.0), writes=[b_QgTs])
        for l in range(DEPTH):
            cload(pool, cclk2, LC[l]["wsT"][:], wsT_s[l].rearrange("g s t -> s g t"))
            dve.op(lambda e, l=l: e.tensor_tensor(out=LC[l]["wsT"][:], in0=LC[l]["wsT"][:],
                                                  in1=bcast(smask["s"][:], [[0, 8], [1, 128]]), op=ALU.mult),
                   reads=[b_const], writes=[b_const], partial=True)
        compute_mods(cs_)
        run_group([dict(ti=None, slot=0)], "s")
    fw.finish()
    return nc, fw, worder_out


def _prep_inputs(inputs, nt):
    f = lambda a: np.ascontiguousarray(np.asarray(a, dtype=np.float32))
    I = {k: np.asarray(v) for k, v in inputs.items()}
    consts = _consts(nt)
    shared = {}
    for k in ("norm_g", "w_ada", "b_ada", "w_in", "q_norm_g", "k_norm_g", "sinks", "sg_norm_g", "hgrn_lb",
              "hgrn_norm_g", "w_br_a", "w_br_b", "w_br_c", "w_out"):
        shared[k] = f(I[k])
    ws = I["w_spatial"]
    shared["wsT_p"] = f(ws.transpose(0, 1, 3, 2))
    shared["wsT_s"] = f(np.tile(ws[:, :, :8, :8].transpose(0, 1, 3, 2), (1, 1, 16, 16)))
    bs = I["b_spatial"]
    shared["bsp_p"] = f(bs.transpose(0, 2, 1))
    shared["bsp_s"] = f(np.tile(bs[:, :, :8].transpose(0, 2, 1), (1, 16, 1)))
    for k, v in consts.items():
        shared["c_" + k] = f(v)
    maps = []
    for c in range(8):
        m = dict(shared)
        m["xp"] = f(I["x_prompt"][c, :nt * 128])
        m["xs"] = f(I["x_sample"][16 * c:16 * c + 16].reshape(128, D))
        m["cp"] = f(np.broadcast_to(I["c_prompt"][c][None, :], (128, D)))
        m["cs"] = f(np.repeat(I["c_sample"][16 * c:16 * c + 16], 8, axis=0))
        ckc = I["cache_win_k"][:, 16 * c:16 * c + 16]
        m["ckT"] = f(ckc.transpose(0, 1, 3, 4, 2))
        m["ck"] = f(ckc.reshape(DEPTH, 16, 128, 128))
        m["cv"] = f(I["cache_win_v"][:, 16 * c:16 * c + 16].reshape(DEPTH, 16, 128, 128))
        m["sth"] = f(I["state_hgrn"][:, 16 * c:16 * c + 16])
        maps.append(m)
    return maps


def _assemble(results, nt):
    yp = np.stack([r["yp"] for r in results]).reshape(8, nt * 128, D)
    ys = np.concatenate([r["ys"].reshape(16, 8, D) for r in results], 0)
    wkp = np.stack([r["wkp"] for r in results], 1).reshape(DEPTH, 8, 128, 2, 64)
    wvp = np.stack([r["wvp"] for r in results], 1).reshape(DEPTH, 8, 128, 2, 64)
    hgp = np.stack([r["hgp"] for r in results], 1)
    wks = np.concatenate([r["wks"] for r in results], 1).reshape(DEPTH, 128, 128, 2, 64)
    wvs = np.concatenate([r["wvs"] for r in results], 1).reshape(DEPTH, 128, 128, 2, 64)
    hgs = np.concatenate([r["hgs"] for r in results], 1)
    sgv = np.concatenate([r["sgv"].reshape(DEPTH, 16, 8, 512) for r in results], 1)
    return (yp, ys, wkp, wvp, hgp, wks, wvs, hgs, sgv)


def kernel(**inputs):
    nt = NTILES
    _, _, order = build_program(nt=nt, G=3, do_sample=True)
    nc, _, _ = build_program(nt=nt, G=3, do_sample=True, worder=order)
    maps = _prep_inputs(inputs, nt)
    res = run_bass_kernel_spmd(nc, maps, core_ids=list(range(8)))
    outs = _assemble(res.results, nt)
    return tuple(np.ascontiguousarray(o.astype(np.float32)) for o in outs)
```
